# Optimizing a Trainium2 kernel written in Bass

```python
import math
import jax, jax.numpy as jnp
from jax import lax
import numpy as np

D_MODEL = 2048
BATCH = 1
SEQ = 16384
DEPTH = 1

HEAD_DIM = 128
N_Q_HEADS = 16
N_KV_HEADS = 4
Q_PER_KV = N_Q_HEADS // N_KV_HEADS
ATTN_WIDTH = N_Q_HEADS * HEAD_DIM
KV_WIDTH = N_KV_HEADS * HEAD_DIM
WINDOW = 128
BLOCK = 128
POOL_WINDOWS = (2, 4, 8, 16)
N_POOL_GROUPS = len(POOL_WINDOWS)
POOL_GROUP_DIM = 256
POOL_WIDTH = N_POOL_GROUPS * POOL_GROUP_DIM
D_FF = 5632
ALPHA = (2.0 * DEPTH) ** 0.25
BETA = (8.0 * DEPTH) ** -0.25
LN_EPS = 1e-5
NEG_INF = -1e30
IN_WIDTH = ATTN_WIDTH + 2 * KV_WIDTH + POOL_WIDTH + 2 * D_MODEL

kernel_name = "hybrid_swa_pool_macaron_deepnorm"


def layer_norm(x, g, b):
    xf = x.astype(jnp.float32)
    mu = jnp.mean(xf, axis=-1, keepdims=True)
    xc = xf - mu
    var = jnp.mean(xc * xc, axis=-1, keepdims=True)
    y = xc * lax.rsqrt(var + LN_EPS)
    return (y * g.astype(jnp.float32) + b.astype(jnp.float32)).astype(x.dtype)


def swiglu(x, w_gate, w_up, w_down):
    return (jax.nn.silu(x @ w_gate) * (x @ w_up)) @ w_down


def alibi_slopes(n_heads):
    return jnp.exp2(-8.0 * jnp.arange(1, n_heads + 1, dtype=jnp.float32) / n_heads)


def banded_gqa_attention(q, k, v, sink):
    B, S = q.shape[0], q.shape[1]
    nb = S // BLOCK
    qb = q.reshape(B, nb, BLOCK, N_KV_HEADS, Q_PER_KV, HEAD_DIM)
    pad = ((0, 0), (BLOCK, BLOCK), (0, 0), (0, 0))
    kp = jnp.pad(k, pad).reshape(B, nb + 2, BLOCK, N_KV_HEADS, HEAD_DIM)
    vp = jnp.pad(v, pad).reshape(B, nb + 2, BLOCK, N_KV_HEADS, HEAD_DIM)
    kb = jnp.concatenate([kp[:, :-2], kp[:, 1:-1], kp[:, 2:]], axis=2)
    vb = jnp.concatenate([vp[:, :-2], vp[:, 1:-1], vp[:, 2:]], axis=2)
    scale = 1.0 / math.sqrt(HEAD_DIM)
    scores = jnp.einsum('bnqhgd,bnkhd->bnhgqk', qb, kb,
                        preferred_element_type=jnp.float32) * scale
    a = jnp.arange(BLOCK)[:, None]
    c = jnp.arange(3 * BLOCK)[None, :]
    dist = jnp.abs(a + BLOCK - c)
    slopes = alibi_slopes(N_Q_HEADS).reshape(N_KV_HEADS, Q_PER_KV)
    bias = -slopes[:, :, None, None] * dist.astype(jnp.float32)[None, None]
    k_pos = (jnp.arange(nb)[:, None] - 1) * BLOCK + jnp.arange(3 * BLOCK)[None, :]
    in_range = (k_pos >= 0) & (k_pos < S)
    valid = in_range[:, None, :] & (dist <= WINDOW)[None]
    scores = jnp.where(valid[None, :, None, None], scores + bias[None, None], NEG_INF)
    sink_l = sink.astype(jnp.float32).reshape(N_KV_HEADS, Q_PER_KV)[None, None, :, :, None, None]
    m = jnp.maximum(jnp.max(scores, axis=-1, keepdims=True), sink_l)
    p = jnp.exp(scores - m)
    denom = jnp.sum(p, axis=-1, keepdims=True) + jnp.exp(sink_l - m)
    probs = (p / denom).astype(v.dtype)
    o = jnp.einsum('bnhgqk,bnkhd->bnqhgd', probs, vb)
    return o.reshape(B, S, ATTN_WIDTH)


def multiscale_pool(p, w_groups, scale):
    B, S, _ = p.shape
    pf = p.astype(jnp.float32)
    cs = jnp.concatenate([jnp.zeros((B, 1, POOL_WIDTH), jnp.float32), jnp.cumsum(pf, axis=1)], axis=1)
    t = jnp.arange(S)
    outs = []
    for g, w in enumerate(POOL_WINDOWS):
        lo = jnp.clip(t - w // 2, 0, S)
        hi = jnp.clip(t + w - w // 2, 0, S)
        csg = cs[..., g * POOL_GROUP_DIM:(g + 1) * POOL_GROUP_DIM]
        win_sum = jnp.take(csg, hi, axis=1) - jnp.take(csg, lo, axis=1)
        mean = win_sum / (hi - lo).astype(jnp.float32)[None, :, None]
        outs.append(mean - pf[..., g * POOL_GROUP_DIM:(g + 1) * POOL_GROUP_DIM])
    d = jnp.stack(outs, axis=2).astype(p.dtype)
    mixed = jnp.einsum('bsgc,gcd->bsgd', d, w_groups).reshape(B, S, POOL_WIDTH)
    return mixed * scale


def hybrid_mixer(x, w_in, attn_sink, pool_w_groups, pool_scale, w_proj_attn, w_proj_pool, w_out):
    B, S, _ = x.shape
    h = x @ w_in
    i0 = ATTN_WIDTH
    i1 = i0 + KV_WIDTH
    i2 = i1 + KV_WIDTH
    i3 = i2 + POOL_WIDTH
    i4 = i3 + D_MODEL
    q = h[..., :i0].reshape(B, S, N_Q_HEADS, HEAD_DIM)
    k = h[..., i0:i1].reshape(B, S, N_KV_HEADS, HEAD_DIM)
    v = h[..., i1:i2].reshape(B, S, N_KV_HEADS, HEAD_DIM)
    pb = h[..., i2:i3]
    gate_a = h[..., i3:i4]
    gate_b = h[..., i4:]
    y_a = banded_gqa_attention(q, k, v, attn_sink) @ w_proj_attn
    y_b = multiscale_pool(pb, pool_w_groups, pool_scale) @ w_proj_pool
    merged = jax.nn.sigmoid(gate_a) * y_a + jax.nn.sigmoid(gate_b) * y_b
    return merged @ w_out


def setup_inputs(seed: int = 0) -> dict:
    key = jax.random.key(seed)
    ks = jax.random.split(key, 24)
    f32 = jnp.float32

    def nrm(k, shape, s):
        return jax.random.normal(k, shape, f32) * s

    L = DEPTH
    D = D_MODEL
    return {
        'x': jax.random.normal(ks[0], (BATCH, SEQ, D), f32),
        'ffn1_w_gate': nrm(ks[1], (L, D, D_FF), D ** -0.5),
        'ffn1_w_up': nrm(ks[2], (L, D, D_FF), D ** -0.5),
        'ffn1_w_down': nrm(ks[3], (L, D_FF, D), BETA * D_FF ** -0.5),
        'ln1_g': 1.0 + nrm(ks[4], (L, D), 0.05),
        'ln1_b': nrm(ks[5], (L, D), 0.02),
        'w_in': nrm(ks[6], (L, D, IN_WIDTH), D ** -0.5),
        'attn_sink': nrm(ks[7], (L, N_Q_HEADS), 0.5),
        'pool_w_groups': nrm(ks[8], (L, N_POOL_GROUPS, POOL_GROUP_DIM, POOL_GROUP_DIM), POOL_GROUP_DIM ** -0.5),
        'pool_scale': 1.0 + nrm(ks[9], (L, POOL_WIDTH), 0.1),
        'w_proj_attn': nrm(ks[10], (L, ATTN_WIDTH, D), ATTN_WIDTH ** -0.5),
        'w_proj_pool': nrm(ks[11], (L, POOL_WIDTH, D), POOL_WIDTH ** -0.5),
        'w_out': nrm(ks[12], (L, D, D), BETA * D ** -0.5),
        'ln2_g': 1.0 + nrm(ks[13], (L, D), 0.05),
        'ln2_b': nrm(ks[14], (L, D), 0.02),
        'ffn2_w_gate': nrm(ks[15], (L, D, D_FF), D ** -0.5),
        'ffn2_w_up': nrm(ks[16], (L, D, D_FF), D ** -0.5),
        'ffn2_w_down': nrm(ks[17], (L, D_FF, D), BETA * D_FF ** -0.5),
        'ln3_g': 1.0 + nrm(ks[18], (L, D), 0.05),
        'ln3_b': nrm(ks[19], (L, D), 0.02),
    }


def reference(x, ffn1_w_gate, ffn1_w_up, ffn1_w_down, ln1_g, ln1_b,
              w_in, attn_sink, pool_w_groups, pool_scale, w_proj_attn, w_proj_pool, w_out,
              ln2_g, ln2_b, ffn2_w_gate, ffn2_w_up, ffn2_w_down, ln3_g, ln3_b):
    for l in range(DEPTH):
        x = layer_norm(ALPHA * x + 0.5 * swiglu(x, ffn1_w_gate[l], ffn1_w_up[l], ffn1_w_down[l]),
                       ln1_g[l], ln1_b[l])
        x = layer_norm(ALPHA * x + hybrid_mixer(x, w_in[l], attn_sink[l], pool_w_groups[l], pool_scale[l],
                                                w_proj_attn[l], w_proj_pool[l], w_out[l]),
                       ln2_g[l], ln2_b[l])
        x = layer_norm(ALPHA * x + 0.5 * swiglu(x, ffn2_w_gate[l], ffn2_w_up[l], ffn2_w_down[l]),
                       ln3_g[l], ln3_b[l])
    return x
```

```python
import math
from contextlib import ExitStack

import numpy as np
import concourse.bass as bass
import concourse.mybir as mybir
from concourse.bass_utils import run_bass_kernel_spmd

F32 = mybir.dt.float32
BF16 = mybir.dt.bfloat16
AF = mybir.ActivationFunctionType
ALU = mybir.AluOpType

NCORES = 8
D = 2048
DC = 16
SEQ = 16384
TOWN = SEQ // NCORES
HALO = 128
TPAD = TOWN + 2 * HALO
DFF = 5632
FC = 44
INW = 8192
ALPHA = 2.0 ** 0.25
LN_EPS = 1e-5
EPS_P = LN_EPS / (ALPHA * ALPHA)
C_FFN = 0.5 / ALPHA
C_MIX = 1.0 / ALPHA
SCALE = 1.0 / math.sqrt(128.0)
NEG_BIG = -1.0e6

C_LN = 0
C_PSC = 96
C_SINK = 104
C_SLOPE = 120
C_KMASK = 136
C_PMASK = 154
C_INVC = 156
C_NEGD = 220
C_SLOPE2 = 604
NCST = 620

SAME_ENG_SYNC = True


class Buf:
    __slots__ = ("name", "lastw", "reads", "dsem", "dcount")

    def __init__(self, name):
        self.name = name
        self.lastw = None
        self.reads = {}
        self.dsem = None
        self.dcount = 0


class Eng:
    def __init__(self, name):
        self.name = name
        self.sem = None
        self.count = 0
        self.pending = False
        self.ops = []
        self.seen = {}


class Prog:
    def __init__(self, nc, stack):
        self.nc = nc
        self.stack = stack
        self.engs = {n: Eng(n) for n in ("pe", "act", "dve", "pool", "sp")}
        for n, e in self.engs.items():
            e.sem = stack.enter_context(nc.semaphore("s_" + n))
        self.all_sems = {}
        self.nsem = 0

    def _note(self, sem, val):
        k = id(sem)
        if k not in self.all_sems or self.all_sems[k][1] < val:
            self.all_sems[k] = (sem, val)

    def _waits(self, eng, reads, writes):
        need = {}

        def add(tok):
            if tok is None:
                return
            sem, val = tok
            k = id(sem)
            if k not in need or need[k][1] < val:
                need[k] = (sem, val)

        for b in reads:
            add(b.lastw)
        for b in writes:
            add(b.lastw)
            for k, tok in b.reads.items():
                add(tok)
        out = []
        for k, (sem, val) in need.items():
            if sem is eng.sem and not (SAME_ENG_SYNC and eng.name != "pe"):
                continue
            if eng.seen.get(k, 0) >= val:
                continue
            eng.seen[k] = val
            out.append((sem, val))
        return out

    def _commit(self, tok, reads, writes):
        k = id(tok[0])
        for b in reads:
            if b.reads.get(k, (None, 0))[1] < tok[1]:
                b.reads[k] = tok
        for b in writes:
            b.lastw = tok
            b.reads = {}
        self._note(*tok)

    def op(self, engname, fn, reads=(), writes=(), signal=True):
        eng = self.engs[engname]
        waits = self._waits(eng, reads, writes)
        if signal:
            eng.count += 1
            eng.pending = False
            tok = (eng.sem, eng.count)
            inc = (eng.sem, 1)
        else:
            eng.pending = True
            tok = (eng.sem, eng.count + 1)
            inc = None
        eng.ops.append((waits, fn, inc))
        self._commit(tok, reads, writes)
        return tok

    def dma(self, engname, fn, owner, reads=(), writes=()):
        eng = self.engs[engname]
        if owner.dsem is None:
            owner.dsem = self.stack.enter_context(self.nc.semaphore("d_%d" % self.nsem))
            self.nsem += 1
        waits = self._waits(eng, reads, writes)
        owner.dcount += 16
        tok = (owner.dsem, owner.dcount)
        eng.ops.append((waits, fn, (owner.dsem, 16)))
        self._commit(tok, reads, writes)
        return tok

    def barrier(self):
        for e in self.engs.values():
            assert not e.pending, e.name
        toks = list(self.all_sems.values())
        for e in self.engs.values():
            waits = []
            for sem, val in toks:
                if sem is e.sem:
                    continue
                if e.seen.get(id(sem), 0) >= val:
                    continue
                e.seen[id(sem)] = val
                waits.append((sem, val))
            if waits:
                e.ops.append((waits, None, None))

    def replay(self, engname, handle):
        for waits, fn, inc in self.engs[engname].ops:
            for sem, val in waits:
                handle.wait_ge(sem, val)
            if fn is not None:
                ins = fn(handle)
                if inc is not None:
                    ins.then_inc(inc[0], inc[1])


def build_program():
    nc = bass.Bass("TRN2", target_bir_lowering=False)
    dt = nc.dram_tensor
    xT = dt("xT", [D, TPAD], F32, kind="ExternalInput").ap()
    cstd = dt("cst", [128, NCST], F32, kind="ExternalInput").ap()
    w1g = dt("ffn1_w_gate", [D, DFF], F32, kind="ExternalInput").ap()
    w1u = dt("ffn1_w_up", [D, DFF], F32, kind="ExternalInput").ap()
    w1d = dt("ffn1_w_down", [DFF, D], F32, kind="ExternalInput").ap()
    w_in = dt("w_in", [D, INW], F32, kind="ExternalInput").ap()
    pwg = dt("pool_w_groups", [4, 256, 256], F32, kind="ExternalInput").ap()
    wpa = dt("w_proj_attn", [D, D], F32, kind="ExternalInput").ap()
    wpp = dt("w_proj_pool", [1024, D], F32, kind="ExternalInput").ap()
    wout = dt("w_out", [D, D], F32, kind="ExternalInput").ap()
    w2g = dt("ffn2_w_gate", [D, DFF], F32, kind="ExternalInput").ap()
    w2u = dt("ffn2_w_up", [D, DFF], F32, kind="ExternalInput").ap()
    w2d = dt("ffn2_w_down", [DFF, D], F32, kind="ExternalInput").ap()
    outT = dt("outT", [D, TOWN], F32, kind="ExternalOutput").ap()
    x1T = dt("x1T", [D, TPAD], F32, kind="Internal").ap()
    x1bT = dt("x1bT", [D, TPAD], BF16, kind="Internal").ap()
    kTd = dt("kTd", [512, TPAD], BF16, kind="Internal").ap()
    Vd = dt("Vd", [TPAD, 512], BF16, kind="Internal").ap()
    pTd = dt("pTd", [1024, TPAD], F32, kind="Internal").ap()
    mixTd = dt("mixTd", [1024, TOWN], BF16, kind="Internal").ap()
    attnTd = dt("attnTd", [D, TOWN], BF16, kind="Internal").ap()
    mrgTd = dt("mrgTd", [D, TOWN], BF16, kind="Internal").ap()
    x2T = dt("x2T", [D, TOWN], F32, kind="Internal").ap()
    x2bT = dt("x2bT", [D, TOWN], BF16, kind="Internal").ap()

    def fm(ap2d, c0, nchunk, t0, nt):
        return ap2d[c0 * 128:(c0 + nchunk) * 128, t0:t0 + nt].rearrange("(c p) t -> p c t", p=128)

    ARENA_W = 51200
    with ExitStack() as stack:
        ec = stack.enter_context
        arena = ec(nc.sbuf_tensor("arena", [128, ARENA_W], F32))
        cst = ec(nc.sbuf_tensor("cstsb", [128, NCST], F32))
        sexp = ec(nc.sbuf_tensor("sexp", [128, 16], F32))
        onesS = ec(nc.sbuf_tensor("onesS", [128, 128], BF16))
        ones1 = ec(nc.sbuf_tensor("ones1", [128, 128], BF16))
        pwsb = ec(nc.sbuf_tensor("pwsb", [128, 4, 2, 256], BF16))
        ps = ec(nc.psum_tensor("ps", [128, 8, 512], F32))
        P = Prog(nc, stack)

        def carve(off_words, nelem, dtype):
            nw = nelem if dtype == F32 else (nelem + 1) // 2
            assert off_words + nw <= ARENA_W, (off_words, nw)
            a = arena[:, off_words:off_words + nw]
            if dtype != F32:
                a = a.bitcast(dtype)
            return a, off_words + nw

        NSLOT = 6
        SLOT_W = 2048
        slot_base = ARENA_W - NSLOT * SLOT_W
        slots = []
        for i in range(NSLOT):
            a, _ = carve(slot_base + i * SLOT_W, 4096, BF16)
            slots.append((a, Buf("slot%d" % i)))
        slot_ctr = [0]
        TMP_W = 512 * 4 + 1024 * 2 + 1024 * 2
        tmp_base = slot_base - TMP_W
        o = tmp_base
        tmpf = []
        for i in range(4):
            a, o = carve(o, 512, F32)
            tmpf.append((a, Buf("tmpf%d" % i)))
        zbq = []
        for i in range(4):
            a, o = carve(o, 1024, BF16)
            zbq.append((a, Buf("zbq%d" % i)))
        mean_sb, o = carve(o, 1024, F32)
        rstd_sb, o = carve(o, 1024, F32)
        mean_b = Buf("mean")
        rstd_b = Buf("rstd")
        assert o == slot_base
        MAIN_W = tmp_base
        tmp_ctr = [0]

        banks = [Buf("bank%d" % i) for i in range(8)]
        bank_ctr = [0]

        reserved = set()

        def next_bank():
            while True:
                i = bank_ctr[0] % 8
                bank_ctr[0] += 1
                if i not in reserved:
                    break
            return ps[:, i, :], banks[i]

        def reserve_bank():
            ap_, b_ = next_bank()
            reserved.add(banks.index(b_))
            return ap_, b_

        def next_tmp():
            i = tmp_ctr[0] % 4
            tmp_ctr[0] += 1
            return tmpf[i]

        cst_b = Buf("cst")
        misc_b = Buf("misc")

        xslots = []
        for i in range(2):
            a, _ = carve(tmp_base + 2048 + i * SLOT_W, 4096, BF16)
            xslots.append((a, Buf("xslot%d" % i)))
        ring = [slots]

        prefetched = {}

        def prefetch_w(w2d, r0, kc, c0, ncol=256):
            key = (w2d.name, r0, kc, c0, ncol)
            assert key not in prefetched
            prefetched[key] = load_w(w2d, r0, kc, c0, ncol)

        def load_w(w2d, r0, kc, c0, ncol=256):
            key = (w2d.name, r0, kc, c0, ncol)
            if key in prefetched:
                return prefetched.pop(key)
            assert kc * ncol <= 4096
            a, b = ring[0][slot_ctr[0] % len(ring[0])]
            slot_ctr[0] += 1
            dst = a[:, 0:kc * ncol].rearrange("p (k n) -> p k n", k=kc)
            src = w2d[r0 * 128:(r0 + kc) * 128, c0:c0 + ncol].rearrange("(k p) n -> p k n", p=128)
            P.dma("pool", lambda g, dst=dst, src=src: g.dma_start(out=dst, in_=src), b, writes=[b])
            return dst, b

        def acc(parts, rhs_fn, n_out, extra_reads, U):
            bank_ap, bank_b = next_bank()
            total = sum(kc for _, _, kc in parts)
            idx = 0
            for (sa, sb, kc) in parts:
                for k in range(kc):
                    lhsT = sa[:, k, n_out * 128:(n_out + 1) * 128]
                    rhs = rhs_fn(idx)
                    first = idx == 0
                    last = idx == total - 1
                    P.op("pe", lambda t, o=bank_ap[:, 0:U], l=lhsT, r=rhs, f=first, la=last:
                         t.matmul(o, l, r, start=f, stop=la),
                         reads=[sb] + list(extra_reads), writes=[bank_b], signal=last)
                    idx += 1
            return bank_ap[:, 0:U], bank_b

        P.dma("sp", lambda s: s.dma_start(out=cst[:], in_=cstd[:, :]), cst_b, writes=[cst_b])
        P.op("dve", lambda v: v.memset(onesS[:], 1.0 / 2048.0), writes=[misc_b])
        P.op("dve", lambda v: v.memset(ones1[:], 1.0), writes=[misc_b])
        P.op("act", lambda a: a.activation(sexp[:], cst[:, C_SINK:C_SINK + 16], AF.Exp),
             reads=[cst_b], writes=[misc_b])
        pw_b = Buf("pw")
        P.dma("pool", lambda g: g.dma_start(out=pwsb[:], in_=pwg.rearrange("g (k p) n -> p g k n", p=128)),
              pw_b, writes=[pw_b])

        def ccol(c):
            return cst[:, c:c + 1]

        def ln_begin(units):
            nu = len(units)
            sb = [reserve_bank() for _ in range(nu)]
            qb = [reserve_bank() for _ in range(nu)]
            return (sb, qb)

        def ln_stats(st, ZF, zf_b, c, Tt, units):
            sb, qb = st
            za, zab = zbq[(2 * c) % 4]
            qa, qab = zbq[(2 * c + 1) % 4]
            P.op("act", lambda a, o=za[:, 0:Tt], i=ZF[:, c, :]: a.activation(o, i, AF.Copy),
                 reads=[zf_b[c]], writes=[zab])
            P.op("act", lambda a, o=qa[:, 0:Tt], i=ZF[:, c, :]: a.activation(o, i, AF.Square),
                 reads=[zf_b[c]], writes=[qab])
            for ui, (u0, U) in enumerate(units):
                last = (c == DC - 1)
                P.op("pe", lambda t, o=sb[ui][0][:, 0:U], r=za[:, u0:u0 + U], f=(c == 0), la=last:
                     t.matmul(o, onesS[:], r, start=f, stop=la),
                     reads=[zab, misc_b], writes=[sb[ui][1]], signal=True)
                P.op("pe", lambda t, o=qb[ui][0][:, 0:U], r=qa[:, u0:u0 + U], f=(c == 0), la=last:
                     t.matmul(o, onesS[:], r, start=f, stop=la),
                     reads=[qab, misc_b], writes=[qb[ui][1]], signal=True)

        def ln_finish_gen(st, ZF, zf_b, Tt, units, lncol, YB, yb_b, on_chunk=None):
            sb, qb = st
            for ui, (u0, U) in enumerate(units):
                m = mean_sb[:, u0:u0 + U]
                r = rstd_sb[:, u0:u0 + U]
                P.op("act", lambda a, o=m, i=sb[ui][0][:, 0:U]: a.activation(o, i, AF.Copy),
                     reads=[sb[ui][1]], writes=[mean_b])
                P.op("dve", lambda v, o=r, i=m: v.tensor_tensor(o, i, i, ALU.mult),
                     reads=[mean_b], writes=[rstd_b])
                P.op("dve", lambda v, o=r, i=qb[ui][0][:, 0:U]:
                     v.scalar_tensor_tensor(o, i, EPS_P, o, ALU.add, ALU.subtract),
                     reads=[qb[ui][1], rstd_b], writes=[rstd_b])
                P.op("act", lambda a, o=r: a.activation(o, o, AF.Sqrt), reads=[rstd_b], writes=[rstd_b])
                P.op("dve", lambda v, o=r: v.reciprocal(o, o), reads=[rstd_b], writes=[rstd_b])
            for ap_, b_ in sb + qb:
                reserved.discard(banks.index(b_))
            yield
            def e_sub(c):
                z = ZF[:, c, :]
                P.op("dve", lambda v, z=z: v.tensor_tensor(z, z, mean_sb[:, 0:Tt], ALU.subtract),
                     reads=[zf_b[c], mean_b], writes=[zf_b[c]])

            e_sub(0)
            for c in range(DC):
                z = ZF[:, c, :]
                if c + 1 < DC:
                    e_sub(c + 1)
                P.op("dve", lambda v, z=z: v.tensor_tensor(z, z, rstd_sb[:, 0:Tt], ALU.mult),
                     reads=[zf_b[c], rstd_b], writes=[zf_b[c]])
                g = ccol(lncol + c)
                b = ccol(lncol + 16 + c)
                P.op("act", lambda a, z=z, g=g, b=b: a.activation(z, z, AF.Identity, bias=b, scale=g),
                     reads=[zf_b[c], cst_b], writes=[zf_b[c]])
                if on_chunk is not None:
                    on_chunk(c)
                if YB is not None and c >= 1:
                    P.op("act", lambda a, o=YB[:, c - 1, :], z=ZF[:, c - 1, :]: a.activation(o, z, AF.Copy),
                         reads=[zf_b[c - 1]], writes=[yb_b])
                yield
            if YB is not None:
                P.op("act", lambda a, o=YB[:, DC - 1, :], z=ZF[:, DC - 1, :]: a.activation(o, z, AF.Copy),
                     reads=[zf_b[DC - 1]], writes=[yb_b])

        def ln_finish(*a, **k):
            for _ in ln_finish_gen(*a, **k):
                pass

        def ffn_phase(tiles, xf_d, xb_d, xb_cast, wg, wu, wd, lncol, outf_d, outb_d, hook, xb_preloaded=False):
            TtM = max(sum(U for _, U in units) for _, units in tiles)
            o = 0
            XBf_, o = carve(o, DC * TtM, BF16)
            ZFf_, o = carve(o, DC * TtM, F32)
            o_hook = o
            HTf_, o = carve(o, 22 * TtM, BF16)
            assert o <= MAIN_W, (o, MAIN_W)
            xb_b = Buf("xb")
            zf_ld = Buf("zfld")
            zf_b = [Buf("zf%d" % c) for c in range(DC)]
            ht_b = [Buf("ht%d" % c) for c in range(22)]
            st_b = Buf("zfst")
            hs_b = [Buf("hs%d" % i) for i in range(3)]
            def xb_view(Tt):
                return XBf_[:, 0:DC * Tt].rearrange("p (c t) -> p c t", c=DC)

            def xb_load(ti):
                t0_, units_ = tiles[ti]
                Tt_ = sum(U for _, U in units_)
                P.dma("sp", lambda s, o_=xb_view(Tt_), i_=fm(xb_d, 0, DC, t0_, Tt_): s.dma_start(out=o_, in_=i_),
                      xb_b, writes=[xb_b])

            def zf_load(ti):
                t0_, units_ = tiles[ti]
                Tt_ = sum(U for _, U in units_)
                ZF_ = ZFf_[:, 0:DC * Tt_].rearrange("p (c t) -> p c t", c=DC)
                P.dma("sp", lambda s, o_=ZF_, i_=fm(xf_d, 0, DC, t0_, Tt_): s.dma_start(out=o_, in_=i_),
                      zf_ld, writes=zf_b)

            deferred = [None]

            def pump(ti):
                if deferred[0] is not None:
                    try:
                        next(deferred[0])
                    except StopIteration:
                        deferred[0] = None
                        zf_load(ti)

            for ti, (t0, units) in enumerate(tiles):
                Tt = sum(U for _, U in units)
                XB = xb_view(Tt)
                ZF = ZFf_[:, 0:DC * Tt].rearrange("p (c t) -> p c t", c=DC)
                HT = HTf_[:, 0:22 * Tt].rearrange("p (c t) -> p c t", c=22)
                if ti == 0:
                    zf_load(0)
                if xb_cast:
                    for c in range(DC):
                        P.op("dve", lambda v, o_=XB[:, c, :], i_=ZF[:, c, :]: v.tensor_copy(o_, i_),
                             reads=[zf_b[c]], writes=[xb_b])
                elif ti == 0 and not xb_preloaded:
                    xb_load(0)
                for g in range(2):
                    for s in range(11):
                        col0 = g * 2816 + s * 256
                        sg = load_w(wg, 0, 16, col0)
                        su = load_w(wu, 0, 16, col0)
                        for nn in range(2):
                            f = s * 2 + nn
                            for (u0, U) in units:
                                rf = lambda k, u0=u0, U=U: XB[:, k, u0:u0 + U]
                                bg, bgb = acc([(sg[0], sg[1], 16)], rf, nn, [xb_b], U)
                                bu, bub = acc([(su[0], su[1], 16)], rf, nn, [xb_b], U)
                                ta, tb = next_tmp()
                                P.op("act", lambda a, o_=ta[:, 0:U], i_=bg: a.activation(o_, i_, AF.Silu),
                                     reads=[bgb], writes=[tb])
                                P.op("dve", lambda v, o_=HT[:, f, u0:u0 + U], i0=bu, i1=ta[:, 0:U]:
                                     v.tensor_tensor(o_, i0, i1, ALU.mult),
                                     reads=[bub, tb] + hs_b, writes=[ht_b[f]])
                            pump(ti)
                    if g == 1:
                        if (not xb_cast) and ti + 1 < len(tiles):
                            xb_load(ti + 1)
                        lnst = ln_begin(units)
                    for s in range(8):
                        sa = load_w(wd, g * 22, 11, s * 256)
                        sb_ = load_w(wd, g * 22 + 11, 11, s * 256)
                        for nn in range(2):
                            n = s * 2 + nn
                            for (u0, U) in units:
                                rf = lambda k, u0=u0, U=U: HT[:, k, u0:u0 + U]
                                bd, bdb = acc([(sa[0], sa[1], 11), (sb_[0], sb_[1], 11)], rf, nn, ht_b, U)
                                P.op("dve", lambda v, z=ZF[:, n, u0:u0 + U], i0=bd:
                                     v.scalar_tensor_tensor(z, i0, C_FFN, z, ALU.mult, ALU.add),
                                     reads=[bdb, zf_b[n]], writes=[zf_b[n]])
                            if g == 1 and n >= 2:
                                ln_stats(lnst, ZF, zf_b, n - 2, Tt, units)
                ln_stats(lnst, ZF, zf_b, DC - 2, Tt, units)
                ln_stats(lnst, ZF, zf_b, DC - 1, Tt, units)
                def store_chunk(c, ZF=ZF, t0=t0, Tt=Tt):
                    P.dma("sp", lambda s, i_=ZF[:, c, :],
                          o_=outf_d[0][c * 128:(c + 1) * 128, outf_d[1] + t0:outf_d[1] + t0 + Tt]:
                          s.dma_start(out=o_, in_=i_), st_b, reads=[zf_b[c]], writes=[outf_d[2]])

                assert deferred[0] is None
                if hook is None and outb_d is None and ti + 1 < len(tiles):
                    deferred[0] = ln_finish_gen(lnst, ZF, zf_b, Tt, units, lncol, None, xb_b, on_chunk=store_chunk)
                    continue
                ln_finish(lnst, ZF, zf_b, Tt, units, lncol, XB if outb_d is not None else None, xb_b,
                          on_chunk=store_chunk)
                if outb_d is not None:
                    P.dma("sp", lambda s, i_=XB, o_=fm(outb_d[0], 0, DC, t0, Tt): s.dma_start(out=o_, in_=i_),
                          xb_b, reads=[xb_b], writes=[outb_d[2]])
                if ti + 1 < len(tiles):
                    zf_load(ti + 1)
                if hook is not None:
                    hook(t0, Tt, units, XB, xb_b, o_hook, hs_b)
            P.barrier()

        kT_db = Buf("kTd")
        V_db = Buf("Vd")
        pT_db = Buf("pTd")
        x1T_db = Buf("x1T")
        x1bT_db = Buf("x1bT")

        kst_b, vst_b, pst_b = Buf("kst"), Buf("vst"), Buf("pst")

        def kvp_hook(t0, Tt, units, XB, xb_b, o2, hs_b):
            KSTf, o2 = carve(o2, 4 * Tt, BF16)
            KST = KSTf.rearrange("p (c t) -> p c t", c=4)
            nblk = Tt // 128
            VSTf, o2 = carve(o2, nblk * 512, BF16)
            VST = VSTf.rearrange("p (b d) -> p b d", b=nblk)
            PSTf, o2 = carve(o2, 8 * Tt, F32)
            PST = PSTf.rearrange("p (c t) -> p c t", c=8)
            assert o2 <= MAIN_W, (o2, MAIN_W)
            for s in range(4):
                sl = load_w(w_in, 0, 16, 3072 + s * 256)
                for nn in range(2):
                    n = s * 2 + nn
                    for (u0, U) in units:
                        rf = lambda k, u0=u0, U=U: XB[:, k, u0:u0 + U]
                        b, bb = acc([(sl[0], sl[1], 16)], rf, nn, [xb_b], U)
                        P.op("act", lambda a, o_=PST[:, n, u0:u0 + U], i_=b: a.activation(o_, i_, AF.Copy),
                             reads=[bb], writes=[pst_b])
            P.dma("sp", lambda s_, i_=PST, o_=fm(pTd, 0, 8, t0, Tt): s_.dma_start(out=o_, in_=i_),
                  pst_b, reads=[pst_b], writes=[pT_db, hs_b[2]])
            for s in range(2):
                sl = load_w(w_in, 0, 16, 2048 + s * 256)
                for nn in range(2):
                    n = s * 2 + nn
                    for (u0, U) in units:
                        rf = lambda k, u0=u0, U=U: XB[:, k, u0:u0 + U]
                        b, bb = acc([(sl[0], sl[1], 16)], rf, nn, [xb_b], U)
                        P.op("act", lambda a, o_=KST[:, n, u0:u0 + U], i_=b: a.activation(o_, i_, AF.Copy),
                             reads=[bb], writes=[kst_b])
            P.dma("sp", lambda s_, i_=KST, o_=fm(kTd, 0, 4, t0, Tt): s_.dma_start(out=o_, in_=i_),
                  kst_b, reads=[kst_b], writes=[kT_db, hs_b[0]])
            for half in range(2):
                sl = load_w(w_in, 0, 16, 2560 + half * 256)
                for bi in range(nblk):
                    bank_ap, bank_b = next_bank()
                    for k in range(16):
                        P.op("pe", lambda t, o_=bank_ap[:, 0:256], l=XB[:, k, bi * 128:(bi + 1) * 128],
                             r=sl[0][:, k, :], f=(k == 0), la=(k == 15): t.matmul(o_, l, r, start=f, stop=la),
                             reads=[sl[1], xb_b], writes=[bank_b], signal=(k == 15))
                    P.op("dve", lambda v, o_=VST[:, bi, half * 256:(half + 1) * 256], i_=bank_ap[:, 0:256]:
                         v.tensor_copy(o_, i_), reads=[bank_b], writes=[vst_b])
            P.dma("sp", lambda s_, i_=VST,
                  o_=Vd[t0:t0 + Tt, :].rearrange("(b p) d -> p b d", p=128): s_.dma_start(out=o_, in_=i_),
                  vst_b, reads=[vst_b], writes=[V_db, hs_b[1]])

        tilesA = [(0, [(0, 384), (384, 384)]), (768, [(0, 384), (384, 384)]), (1536, [(0, 384), (384, 384)])]
        ffn_phase(tilesA, xT, xT, True, w1g, w1u, w1d, C_LN + 0, (x1T, 0, x1T_db), (x1bT, 0, x1bT_db), kvp_hook)

        X1f_b2, _ = carve(MAIN_W - DC * 512, DC * 1024, BF16)
        X1_B2 = X1f_b2.rearrange("p (c t) -> p c t", c=DC)
        x1_b2 = Buf("x1b2")

        def x1_load_b2(h_):
            P.dma("sp", lambda s, o_=X1_B2, i_=fm(x1bT, 0, DC, HALO + h_ * 1024, 1024): s.dma_start(out=o_, in_=i_),
                  x1_b2, reads=[x1bT_db], writes=[x1_b2])

        mix_db = Buf("mixTd")
        for h in range(2):
            o = 0
            PFf, o = carve(o, 8 * 1040, F32)
            PF = PFf.rearrange("p (c t) -> p c t", c=8)
            S1, o = carve(o, 1040, F32)
            S2, o = carve(o, 1040, F32)
            S3, o = carve(o, 1040, F32)
            S4, o = carve(o, 1040, F32)
            DTf, o = carve(o, 8 * 1024, BF16)
            DTt = DTf.rearrange("p (c t) -> p c t", c=8)
            MTf, o = carve(o, 8 * 1024, BF16)
            MT = MTf.rearrange("p (c t) -> p c t", c=8)
            assert o <= MAIN_W
            pf_b = [Buf("pf%d" % c) for c in range(8)]
            pf_ld = Buf("pfld")
            s1_b, s2_b, s3_b, s4_b = Buf("s1"), Buf("s2"), Buf("s3"), Buf("s4")
            dt_b = [Buf("dt%d" % c) for c in range(8)]
            mt_b = Buf("mt")
            P.dma("sp", lambda s, o_=PF, i_=fm(pTd, 0, 8, HALO + h * 1024 - 8, 1040): s.dma_start(out=o_, in_=i_),
                  pf_ld, reads=[pT_db], writes=pf_b)
            for c in range(8):
                if h == 0:
                    P.op("dve", lambda v, a=PF[:, c, 0:8]: v.tensor_scalar(a, a, ccol(C_PMASK), None, ALU.mult),
                         reads=[pf_b[c], cst_b], writes=[pf_b[c]])
                else:
                    P.op("dve", lambda v, a=PF[:, c, 1032:1040]: v.tensor_scalar(a, a, ccol(C_PMASK + 1), None, ALU.mult),
                         reads=[pf_b[c], cst_b], writes=[pf_b[c]])
            for c in range(8):
                gi = c // 2
                w = (2, 4, 8, 16)[gi]
                p = PF[:, c, :]
                en = "dve"
                if c % 2 == 0:
                    SA, sab_, SB, sbb_ = S1, s1_b, S2, s2_b
                else:
                    SA, sab_, SB, sbb_ = S3, s3_b, S4, s4_b
                P.op(en, lambda v, p=p, SA=SA: v.tensor_tensor(SA[:, 0:1039], p[:, 0:1039], p[:, 1:1040], ALU.add),
                     reads=[pf_b[c]], writes=[sab_])
                cur, curb, oth, othb = SA, sab_, SB, sbb_
                n_valid = 1039
                step = 2
                while step < w:
                    nv = n_valid - step
                    P.op(en, lambda v, o_=oth[:, 0:nv], a=cur[:, 0:nv], b=cur[:, step:step + nv]:
                         v.tensor_tensor(o_, a, b, ALU.add), reads=[curb], writes=[othb])
                    cur, curb, oth, othb = oth, othb, cur, curb
                    n_valid = nv
                    step *= 2
                sh = 8 - w // 2
                en = "dve"
                P.op(en, lambda v, o_=DTt[:, c, :], a=cur[:, sh:sh + 1024], p=p[:, 8:1032], w=w:
                     v.scalar_tensor_tensor(o_, a, 1.0 / w, p, ALU.mult, ALU.subtract),
                     reads=[curb, pf_b[c]], writes=[dt_b[c]])
                if h == 0:
                    e0, tcol = 0, C_INVC + gi * 16
                else:
                    e0, tcol = 1016, C_INVC + gi * 16 + 8
                P.op(en, lambda v, o_=oth[:, 0:8], a=cur[:, sh + e0:sh + e0 + 8], t_=cst[:, tcol:tcol + 8]:
                     v.tensor_tensor(o_, a, t_, ALU.mult), reads=[curb, cst_b], writes=[othb])
                P.op(en, lambda v, o_=DTt[:, c, e0:e0 + 8], a=oth[:, 0:8], p=p[:, 8 + e0:16 + e0]:
                     v.tensor_tensor(o_, a, p, ALU.subtract), reads=[othb, pf_b[c], dt_b[c]], writes=[dt_b[c]])
            for gi in range(4):
                for no in range(2):
                    for u in range(2):
                        bank_ap, bank_b = next_bank()
                        for ki in range(2):
                            P.op("pe", lambda t, o_=bank_ap, l=pwsb[:, gi, ki, no * 128:(no + 1) * 128],
                                 r=DTt[:, 2 * gi + ki, u * 512:(u + 1) * 512], f=(ki == 0), la=(ki == 1):
                                 t.matmul(o_, l, r, start=f, stop=la),
                                 reads=[pw_b, dt_b[2 * gi + ki]], writes=[bank_b], signal=(ki == 1))
                        cc = 2 * gi + no
                        P.op("act", lambda a, o_=MT[:, cc, u * 512:(u + 1) * 512], i_=bank_ap, sc=ccol(C_PSC + cc):
                             a.activation(o_, i_, AF.Identity, scale=sc), reads=[bank_b, cst_b], writes=[mt_b])
            P.dma("sp", lambda s, i_=MT, o_=fm(mixTd, 0, 8, h * 1024, 1024): s.dma_start(out=o_, in_=i_),
                  mt_b, reads=[mt_b], writes=[mix_db])
            if h == 1:
                x1_load_b2(0)
                for g_ in range(3):
                    for s_ in range(2):
                        prefetch_w(w_in, 0, 16, g_ * 512 + s_ * 256)
            P.barrier()

        attn_db = Buf("attnTd")
        for h in range(2):
            o = 0
            X1 = X1_B2
            x1_b = x1_b2
            KTf, o = carve(o, 4 * 1280, BF16)
            KT = KTf.rearrange("p (c t) -> p c t", c=4)
            VVf, o = carve(o, 10 * 512, BF16)
            VV = VVf.rearrange("p (b d) -> p b d", b=10)
            ATf, o = carve(o, DC * 1024, BF16)
            AT = ATf.rearrange("p (c t) -> p c t", c=DC)
            QTs = []
            for i in range(2):
                qf, o = carve(o, 4 * 1024, BF16)
                QTs.append((qf.rearrange("p (c t) -> p c t", c=4), Buf("qt%d" % i)))
            PTs = []
            for i in range(8):
                pf_, o = carve(o, 512, BF16)
                PTs.append((pf_, Buf("pt%d" % i)))
            DNs = []
            for i in range(2):
                df_, o = carve(o, 512, F32)
                DNs.append((df_, Buf("dn%d" % i)))
            EGs = []
            for i in range(2):
                egf_, o = carve(o, 3 * 512, F32)
                sxg_, o = carve(o, 512, F32)
                EGs.append((egf_, sxg_, Buf("eg%d" % i), Buf("sxg%d" % i)))
            assert o <= MAIN_W - DC * 512, (o, MAIN_W)
            kt_b, vv_b = Buf("kt"), Buf("vv")
            at_g = [Buf("at%d" % g_) for g_ in range(4)]
            atst_b = Buf("atst")
            P.dma("sp", lambda s, o_=KT, i_=fm(kTd, 0, 4, h * 1024, 1280): s.dma_start(out=o_, in_=i_),
                  kt_b, reads=[kT_db], writes=[kt_b])
            P.dma("sp", lambda s, o_=VV,
                  i_=Vd[h * 1024:h * 1024 + 1280, :].rearrange("(b p) d -> p b d", p=128): s.dma_start(out=o_, in_=i_),
                  vv_b, reads=[V_db], writes=[vv_b])
            pt_ctr = [0]
            qslots = {}

            def qpiece(g, idx):
                QT, qt_b = QTs[g % 2]
                s_, nn, u = idx // 4, (idx // 2) % 2, idx % 2
                sl = qslots[(g, s_)]
                hd = s_ * 2 + nn
                rf = lambda k, u=u: X1[:, k, u * 512:(u + 1) * 512]
                b, bb = acc([(sl[0], sl[1], 16)], rf, nn, [x1_b], 512)
                P.op("act", lambda a, o_=QT[:, hd, u * 512:(u + 1) * 512], i_=b: a.activation(o_, i_, AF.Copy),
                     reads=[bb], writes=[qt_b])

            def qload(g):
                for s_ in range(2):
                    qslots[(g, s_)] = load_w(w_in, 0, 16, g * 512 + s_ * 256)

            qload(0)
            qload(1)
            qload(2)

            def s_stage(g, i):
                QT, qt_b = QTs[g % 2]
                pts = []
                for j in range(3):
                    blk = i + j
                    bank_ap, bank_b = next_bank()
                    bS = bank_ap.rearrange("p (h q) -> p h q", h=4)
                    P.op("pe", lambda t, o_=bS, l=KT[:, g, blk * 128:(blk + 1) * 128],
                         r=QT[:, :, i * 128:(i + 1) * 128]: t.matmul(o_, l, r, start=True, stop=True),
                         reads=[kt_b, qt_b], writes=[bank_b])
                    ta, tb = next_tmp()
                    P.op("act", lambda a, o_=ta, i_=bank_ap, km=ccol(C_KMASK + h * 8 + blk):
                         a.activation(o_, i_, AF.Exp, bias=km, scale=SCALE),
                         reads=[bank_b, cst_b], writes=[tb])
                    pa, pb = PTs[pt_ctr[0] % len(PTs)]
                    pt_ctr[0] += 1
                    P.op("pool" if j < 2 else "dve", lambda g_, o_=pa, i0=ta, i1=EGs[g % 2][0][:, j * 512:(j + 1) * 512]:
                         g_.tensor_tensor(o_, i0, i1, ALU.mult),
                         reads=[tb, EGs[g % 2][2]], writes=[pb])
                    pts.append((pa, pb, blk))
                return pts

            def o_stage(g, i, pts):
                bO_ap, bO_b = next_bank()
                for j, (pa, pb, blk) in enumerate(pts):
                    P.op("pe", lambda t, o_=bO_ap, l=VV[:, blk, g * 128:(g + 1) * 128], r=pa, f=(j == 0), la=(j == 2):
                         t.matmul(o_, l, r, start=f, stop=la),
                         reads=[vv_b, pb], writes=[bO_b], signal=(j == 2))
                bD_ap, bD_b = next_bank()
                for j, (pa, pb, blk) in enumerate(pts):
                    P.op("pe", lambda t, o_=bD_ap, r=pa, f=(j == 0), la=(j == 2):
                         t.matmul(o_, ones1[:], r, start=f, stop=la),
                         reads=[misc_b, pb], writes=[bD_b], signal=(j == 2))
                dn, dnb = DNs[i % 2]
                P.op("dve", lambda v, o_=dn, i0=bD_ap, i1=EGs[g % 2][1]: v.tensor_tensor(o_, i0, i1, ALU.add),
                     reads=[bD_b, EGs[g % 2][3]], writes=[dnb])
                P.op("act", lambda a, o_=dn: a.activation(o_, o_, AF.Ln), reads=[dnb], writes=[dnb])
                P.op("act", lambda a, o_=dn: a.activation(o_, o_, AF.Exp, scale=-1.0), reads=[dnb], writes=[dnb])
                P.op("dve", lambda v, o_=AT[:, g * 4:(g + 1) * 4, i * 128:(i + 1) * 128],
                     i0=bO_ap.rearrange("p (h q) -> p h q", h=4), i1=dn.rearrange("p (h q) -> p h q", h=4):
                     v.tensor_tensor(o_, i0, i1, ALU.mult),
                     reads=[bO_b, dnb], writes=[at_g[g]])
                if i == 7:
                    P.dma("sp", lambda s_, i_=AT[:, g * 4:(g + 1) * 4, :], o_=fm(attnTd, g * 4, 4, h * 1024, 1024):
                          s_.dma_start(out=o_, in_=i_), atst_b, reads=[at_g[g]], writes=[attn_db])

            def eg_build(g):
                egf_, sxg_, eg_b, sxg_b = EGs[g % 2]
                EG = egf_.rearrange("p (j h q) -> p j h q", j=3, h=4)
                SXG3 = sxg_.rearrange("p (h q) -> p h q", h=4)
                for j in range(3):
                    for hd in range(4):
                        P.op("act", lambda a, o_=EG[:, j, hd, :], nd=cst[:, C_NEGD + j * 128:C_NEGD + (j + 1) * 128],
                             sc=ccol(C_SLOPE2 + g * 4 + hd): a.activation(o_, nd, AF.Exp, scale=sc),
                             reads=[cst_b], writes=[eg_b])
                for hd in range(4):
                    P.op("act", lambda a, o_=SXG3[:, hd, :], nd=cst[:, C_NEGD + 128:C_NEGD + 256],
                         se=sexp[:, g * 4 + hd:g * 4 + hd + 1]: a.activation(o_, nd, AF.Identity, bias=se, scale=0.0),
                         reads=[cst_b, misc_b], writes=[sxg_b])

            for idx in range(8):
                qpiece(0, idx)
                if idx == 0:
                    eg_build(0)
            seq = [(g, i) for g in range(4) for i in range(8)]
            pts_next = s_stage(0, 0)
            for si, (g, i) in enumerate(seq):
                pts = pts_next
                if g == 0 and i == 4:
                    qload(3)
                if i == 1 and g < 3:
                    eg_build(g + 1)
                if si + 1 < len(seq):
                    pts_next = s_stage(*seq[si + 1])
                o_stage(g, i, pts)
                if g < 3 and i < 4:
                    qpiece(g + 1, 2 * i)
                    qpiece(g + 1, 2 * i + 1)
                    if g == 2 and i == 3 and h == 0:
                        x1_load_b2(1)
            if h == 0:
                for g_ in range(3):
                    for s_ in range(2):
                        prefetch_w(w_in, 0, 16, g_ * 512 + s_ * 256)
            else:
                ring[0] = slots + xslots
                for s_ in range(2):
                    prefetch_w(wpa, 0, 16, s_ * 256)
                    prefetch_w(wpp, 0, 8, s_ * 256)
                    prefetch_w(w_in, 0, 16, 4096 + s_ * 256)
                    prefetch_w(w_in, 0, 16, 6144 + s_ * 256)
            P.barrier()

        mrg_db = Buf("mrgTd")
        ring[0] = slots + xslots
        for h in range(2):
            o = 0
            ATf, o = carve(o, DC * 1024, BF16)
            AT = ATf.rearrange("p (c t) -> p c t", c=DC)
            MTf, o = carve(o, 8 * 1024, BF16)
            MT = MTf.rearrange("p (c t) -> p c t", c=8)
            X1f, o = carve(o, DC * 1024, BF16)
            X1 = X1f.rearrange("p (c t) -> p c t", c=DC)
            MGf, o = carve(o, DC * 1024, BF16)
            MG = MGf.rearrange("p (c t) -> p c t", c=DC)
            assert o <= MAIN_W
            at_b, mt_b, x1_b = Buf("at"), Buf("mt"), Buf("x1")
            mg_c = [Buf("mg%d" % c) for c in range(DC)]
            mgst_b = Buf("mgst")
            P.dma("sp", lambda s, o_=AT, i_=fm(attnTd, 0, DC, h * 1024, 1024): s.dma_start(out=o_, in_=i_),
                  at_b, reads=[attn_db], writes=[at_b])
            P.dma("sp", lambda s, o_=MT, i_=fm(mixTd, 0, 8, h * 1024, 1024): s.dma_start(out=o_, in_=i_),
                  mt_b, reads=[mix_db], writes=[mt_b])
            P.dma("sp", lambda s, o_=X1, i_=fm(x1bT, 0, DC, HALO + h * 1024, 1024): s.dma_start(out=o_, in_=i_),
                  x1_b, reads=[x1bT_db], writes=[x1_b])
            for s in range(8):
                s_pa = load_w(wpa, 0, 16, s * 256)
                s_pp = load_w(wpp, 0, 8, s * 256)
                s_ga = load_w(w_in, 0, 16, 4096 + s * 256)
                s_gb = load_w(w_in, 0, 16, 6144 + s * 256)
                for nn in range(2):
                    n = s * 2 + nn
                    for u in range(2):
                        us = slice(u * 512, (u + 1) * 512)
                        bya, byab = acc([(s_pa[0], s_pa[1], 16)], lambda k, us=us: AT[:, k, us], nn, [at_b], 512)
                        byb, bybb = acc([(s_pp[0], s_pp[1], 8)], lambda k, us=us: MT[:, k, us], nn, [mt_b], 512)
                        bga, bgab = acc([(s_ga[0], s_ga[1], 16)], lambda k, us=us: X1[:, k, us], nn, [x1_b], 512)
                        bgb, bgbb = acc([(s_gb[0], s_gb[1], 16)], lambda k, us=us: X1[:, k, us], nn, [x1_b], 512)
                        sa, sab = next_tmp()
                        sb2, sbb = next_tmp()
                        P.op("act", lambda a, o_=sa, i_=bga: a.activation(o_, i_, AF.Sigmoid), reads=[bgab], writes=[sab])
                        P.op("act", lambda a, o_=sb2, i_=bgb: a.activation(o_, i_, AF.Sigmoid), reads=[bgbb], writes=[sbb])
                        P.op("dve", lambda v, o_=sa, i0=bya: v.tensor_tensor(o_, i0, o_, ALU.mult),
                             reads=[byab, sab], writes=[sab])
                        P.op("dve", lambda v, o_=sb2, i0=byb: v.tensor_tensor(o_, i0, o_, ALU.mult),
                             reads=[bybb, sbb], writes=[sbb])
                        P.op("dve", lambda v, o_=MG[:, n, us], a_=sa, b_=sb2: v.tensor_tensor(o_, a_, b_, ALU.add),
                             reads=[sab, sbb], writes=[mg_c[n]])
                    P.dma("sp", lambda s_, i_=MG[:, n, :], o_=mrgTd[n * 128:(n + 1) * 128, h * 1024:(h + 1) * 1024]:
                          s_.dma_start(out=o_, in_=i_), mgst_b, reads=[mg_c[n]], writes=[mrg_db])
            if h == 0:
                for s_ in range(2):
                    prefetch_w(wpa, 0, 16, s_ * 256)
                    prefetch_w(wpp, 0, 8, s_ * 256)
                    prefetch_w(w_in, 0, 16, 4096 + s_ * 256)
                    prefetch_w(w_in, 0, 16, 6144 + s_ * 256)
            else:
                ring[0] = slots
                for s_ in range(4):
                    prefetch_w(wout, 0, 16, s_ * 256)
            P.barrier()

        x2T_db, x2bT_db = Buf("x2T"), Buf("x2bT")
        ring[0] = slots
        zf_ld4 = [Buf("zfld%d" % i) for i in range(4)]
        o = 0
        MGf, o = carve(o, DC * 1024, BF16)
        MG = MGf.rearrange("p (c t) -> p c t", c=DC)
        ZFf, o = carve(o, DC * 1024, F32)
        ZF = ZFf.rearrange("p (c t) -> p c t", c=DC)
        YBf, o = carve(o, DC * 1024, BF16)
        YB = YBf.rearrange("p (c t) -> p c t", c=DC)
        assert o <= MAIN_W
        mg_b, yb_b, st_b = Buf("mg"), Buf("yb"), Buf("zfst")
        zf_b = [Buf("zf%d" % c) for c in range(DC)]

        def mg_load(h_):
            P.dma("sp", lambda s, o_=MG, i_=fm(mrgTd, 0, DC, h_ * 1024, 1024): s.dma_start(out=o_, in_=i_),
                  mg_b, reads=[mrg_db], writes=[mg_b])

        mg_load(0)
        for h in range(2):
            for q4 in range(4):
                P.dma("sp" if h == 0 else "pool",
                      lambda s, o_=ZF[:, q4 * 4:(q4 + 1) * 4, :], i_=fm(x1T, q4 * 4, 4, HALO + h * 1024, 1024):
                      s.dma_start(out=o_, in_=i_), zf_ld4[q4], reads=[x1T_db], writes=zf_b[q4 * 4:(q4 + 1) * 4])
            units = [(0, 512), (512, 512)]
            lnst = ln_begin(units)
            for s in range(8):
                sl = load_w(wout, 0, 16, s * 256)
                for nn in range(2):
                    n = s * 2 + nn
                    for (u0, U) in units:
                        b, bb = acc([(sl[0], sl[1], 16)], lambda k, u0=u0, U=U: MG[:, k, u0:u0 + U], nn, [mg_b], U)
                        P.op("dve", lambda v, z=ZF[:, n, u0:u0 + U], i0=b:
                             v.scalar_tensor_tensor(z, i0, C_MIX, z, ALU.mult, ALU.add),
                             reads=[bb, zf_b[n]], writes=[zf_b[n]])
                    if n >= 2:
                        ln_stats(lnst, ZF, zf_b, n - 2, 1024, units)
            if h == 0:
                mg_load(1)
            else:
                xbc, _ = carve(0, DC * 768, BF16)
                P.dma("sp", lambda s, o_=xbc.rearrange("p (c t) -> p c t", c=DC), i_=fm(x2bT, 0, DC, 0, 768):
                      s.dma_start(out=o_, in_=i_), mg_b, reads=[x2bT_db], writes=[mg_b])
            ln_stats(lnst, ZF, zf_b, DC - 2, 1024, units)
            ln_stats(lnst, ZF, zf_b, DC - 1, 1024, units)

            def store_chunk4(c, h=h):
                P.dma("sp", lambda s, i_=ZF[:, c, :], o_=x2T[c * 128:(c + 1) * 128, h * 1024:(h + 1) * 1024]:
                      s.dma_start(out=o_, in_=i_), st_b, reads=[zf_b[c]], writes=[x2T_db])

            ln_finish(lnst, ZF, zf_b, 1024, units, C_LN + 32, YB, yb_b, on_chunk=store_chunk4)
            P.dma("sp", lambda s, i_=YB, o_=fm(x2bT, 0, DC, h * 1024, 1024): s.dma_start(out=o_, in_=i_),
                  yb_b, reads=[yb_b], writes=[x2bT_db])
            if h == 0:
                for s_ in range(4):
                    prefetch_w(wout, 0, 16, s_ * 256)
            else:
                for s_ in range(2):
                    prefetch_w(w2g, 0, 16, s_ * 256)
                    prefetch_w(w2u, 0, 16, s_ * 256)
        P.barrier()

        out_db = Buf("outT")
        tilesC = [(0, [(0, 384), (384, 384)]), (768, [(0, 384), (384, 384)]), (1536, [(0, 256), (256, 256)])]
        ffn_phase(tilesC, x2T, x2bT, False, w2g, w2u, w2d, C_LN + 64, (outT, 0, out_db), None, None,
                  xb_preloaded=True)
        P.barrier()
        assert not prefetched, list(prefetched)

        with nc.Block() as block:
            @block.tensor
            def _(t):
                P.replay("pe", t)

            @block.scalar
            def _(a):
                P.replay("act", a)

            @block.vector
            def _(v):
                P.replay("dve", v)

            @block.gpsimd
            def _(g):
                P.replay("pool", g)

            @block.sync
            def _(s):
                P.replay("sp", s)
    return nc


def _make_cst(core, inp):
    cst = np.zeros((128, NCST), np.float32)

    def fmcol(v):
        return np.ascontiguousarray(v.reshape(-1, 128).T)

    for i, nm in enumerate(("ln1_g", "ln1_b", "ln2_g", "ln2_b", "ln3_g", "ln3_b")):
        cst[:, C_LN + 16 * i:C_LN + 16 * (i + 1)] = fmcol(inp[nm][0])
    cst[:, C_PSC:C_PSC + 8] = fmcol(inp["pool_scale"][0])
    cst[:, C_SINK:C_SINK + 16] = np.broadcast_to(inp["attn_sink"][0][None, :], (128, 16))
    slopes = np.exp2(-8.0 * np.arange(1, 17, dtype=np.float32) / 16.0).astype(np.float32)
    cst[:, C_SLOPE:C_SLOPE + 16] = (slopes / np.float32(SCALE))[None, :]
    cst[:, C_SLOPE2:C_SLOPE2 + 16] = slopes[None, :]
    km = np.zeros(18, np.float32)
    if core == 0:
        km[0] = -30000.0
    if core == NCORES - 1:
        km[17] = -30000.0
    cst[:, C_KMASK:C_KMASK + 18] = km[None, :]
    cst[:, C_PMASK] = 0.0 if core == 0 else 1.0
    cst[:, C_PMASK + 1] = 0.0 if core == NCORES - 1 else 1.0
    for gi, w in enumerate((2, 4, 8, 16)):
        tl = np.arange(8)
        gt = core * TOWN + tl
        lo = np.clip(gt - w // 2, 0, SEQ)
        hi = np.clip(gt + w - w // 2, 0, SEQ)
        cst[:, C_INVC + gi * 16:C_INVC + gi * 16 + 8] = (1.0 / (hi - lo).astype(np.float32))[None, :]
        gt = core * TOWN + TOWN - 8 + tl
        lo = np.clip(gt - w // 2, 0, SEQ)
        hi = np.clip(gt + w - w // 2, 0, SEQ)
        cst[:, C_INVC + gi * 16 + 8:C_INVC + gi * 16 + 16] = (1.0 / (hi - lo).astype(np.float32))[None, :]
    s_ = np.arange(128)[:, None]
    t_ = np.arange(128)[None, :]
    for j in range(3):
        if j == 0:
            dist = t_ - s_ + 128
        elif j == 1:
            dist = np.abs(t_ - s_)
        else:
            dist = s_ + 128 - t_
        nd = np.where(dist <= 128, -dist.astype(np.float32), np.float32(NEG_BIG))
        cst[:, C_NEGD + j * 128:C_NEGD + (j + 1) * 128] = nd
    return cst


_NC_CACHE = {}


def kernel(**inputs):
    inp = {k: np.asarray(v) for k, v in inputs.items()}
    x = inp["x"][0]
    xpad = np.zeros((SEQ + 2 * HALO, D), np.float32)
    xpad[HALO:HALO + SEQ] = x
    if "nc" not in _NC_CACHE:
        _NC_CACHE["nc"] = build_program()
    nc = _NC_CACHE["nc"]
    wnames = ("ffn1_w_gate", "ffn1_w_up", "ffn1_w_down", "w_in", "pool_w_groups", "w_proj_attn",
              "w_proj_pool", "w_out", "ffn2_w_gate", "ffn2_w_up", "ffn2_w_down")
    wts = {n: np.ascontiguousarray(inp[n][0], dtype=np.float32) for n in wnames}
    in_maps = []
    for c in range(NCORES):
        m = dict(wts)
        m["xT"] = np.ascontiguousarray(xpad[c * TOWN:c * TOWN + TPAD].T)
        m["cst"] = _make_cst(c, inp)
        in_maps.append(m)
    res = run_bass_kernel_spmd(nc, in_maps, core_ids=list(range(NCORES)))
    out = np.concatenate([np.ascontiguousarray(r["outT"].T) for r in res.results], axis=0)
    return out.reshape(1, SEQ, D).astype(np.float32)
```

```python
import math
from contextlib import ExitStack

import numpy as np
import concourse.bass as bass
import concourse.mybir as mybir
from concourse.bass_utils import run_bass_kernel_spmd

F32 = mybir.dt.float32
BF16 = mybir.dt.bfloat16
AF = mybir.ActivationFunctionType
ALU = mybir.AluOpType

NCORES = 8
D = 2048
DC = 16
SEQ = 16384
TOWN = SEQ // NCORES
HALO = 128
TPAD = TOWN + 2 * HALO
DFF = 5632
FC = 44
INW = 8192
ALPHA = 2.0 ** 0.25
LN_EPS = 1e-5
EPS_P = LN_EPS / (ALPHA * ALPHA)
C_FFN = 0.5 / ALPHA
C_MIX = 1.0 / ALPHA
SCALE = 1.0 / math.sqrt(128.0)
NEG_BIG = -1.0e6

C_LN = 0
C_PSC = 96
C_SINK = 104
C_SLOPE = 120
C_KMASK = 136
C_PMASK = 154
C_INVC = 156
C_NEGD = 220
C_SLOPE2 = 604
NCST = 620

SAME_ENG_SYNC = True


class Buf:
    __slots__ = ("name", "lastw", "reads", "dsem", "dcount")

    def __init__(self, name):
        self.name = name
        self.lastw = None
        self.reads = {}
        self.dsem = None
        self.dcount = 0


class Eng:
    def __init__(self, name):
        self.name = name
        self.sem = None
        self.count = 0
        self.pending = False
        self.ops = []
        self.seen = {}


class Prog:
    def __init__(self, nc, stack):
        self.nc = nc
        self.stack = stack
        self.engs = {n: Eng(n) for n in ("pe", "act", "dve", "pool", "sp")}
        for n, e in self.engs.items():
            e.sem = stack.enter_context(nc.semaphore("s_" + n))
        self.all_sems = {}
        self.nsem = 0

    def _note(self, sem, val):
        k = id(sem)
        if k not in self.all_sems or self.all_sems[k][1] < val:
            self.all_sems[k] = (sem, val)

    def _waits(self, eng, reads, writes):
        need = {}

        def add(tok):
            if tok is None:
                return
            sem, val = tok
            k = id(sem)
            if k not in need or need[k][1] < val:
                need[k] = (sem, val)

        for b in reads:
            add(b.lastw)
        for b in writes:
            add(b.lastw)
            for k, tok in b.reads.items():
                add(tok)
        out = []
        for k, (sem, val) in need.items():
            if sem is eng.sem and not (SAME_ENG_SYNC and eng.name != "pe"):
                continue
            if eng.seen.get(k, 0) >= val:
                continue
            eng.seen[k] = val
            out.append((sem, val))
        return out

    def _commit(self, tok, reads, writes):
        k = id(tok[0])
        for b in reads:
            if b.reads.get(k, (None, 0))[1] < tok[1]:
                b.reads[k] = tok
        for b in writes:
            b.lastw = tok
            b.reads = {}
        self._note(*tok)

    def op(self, engname, fn, reads=(), writes=(), signal=True):
        eng = self.engs[engname]
        waits = self._waits(eng, reads, writes)
        if signal:
            eng.count += 1
            eng.pending = False
            tok = (eng.sem, eng.count)
            inc = (eng.sem, 1)
        else:
            eng.pending = True
            tok = (eng.sem, eng.count + 1)
            inc = None
        eng.ops.append((waits, fn, inc))
        self._commit(tok, reads, writes)
        return tok

    def dma(self, engname, fn, owner, reads=(), writes=()):
        eng = self.engs[engname]
        if owner.dsem is None:
            owner.dsem = self.stack.enter_context(self.nc.semaphore("d_%d" % self.nsem))
            self.nsem += 1
        waits = self._waits(eng, reads, writes)
        owner.dcount += 16
        tok = (owner.dsem, owner.dcount)
        eng.ops.append((waits, fn, (owner.dsem, 16)))
        self._commit(tok, reads, writes)
        return tok

    def barrier(self):
        for e in self.engs.values():
            assert not e.pending, e.name
        toks = list(self.all_sems.values())
        for e in self.engs.values():
            waits = []
            for sem, val in toks:
                if sem is e.sem:
                    continue
                if e.seen.get(id(sem), 0) >= val:
                    continue
                e.seen[id(sem)] = val
                waits.append((sem, val))
            if waits:
                e.ops.append((waits, None, None))

    def replay(self, engname, handle):
        for waits, fn, inc in self.engs[engname].ops:
            for sem, val in waits:
                handle.wait_ge(sem, val)
            if fn is not None:
                ins = fn(handle)
                if inc is not None:
                    ins.then_inc(inc[0], inc[1])


def build_program():
    nc = bass.Bass("TRN2", target_bir_lowering=False)
    dt = nc.dram_tensor
    xT = dt("xT", [D, TPAD], F32, kind="ExternalInput").ap()
    cstd = dt("cst", [128, NCST], F32, kind="ExternalInput").ap()
    w1g = dt("ffn1_w_gate", [D, DFF], F32, kind="ExternalInput").ap()
    w1u = dt("ffn1_w_up", [D, DFF], F32, kind="ExternalInput").ap()
    w1d = dt("ffn1_w_down", [DFF, D], F32, kind="ExternalInput").ap()
    w_in = dt("w_in", [D, INW], F32, kind="ExternalInput").ap()
    pwg = dt("pool_w_groups", [4, 256, 256], F32, kind="ExternalInput").ap()
    wpa = dt("w_proj_attn", [D, D], F32, kind="ExternalInput").ap()
    wpp = dt("w_proj_pool", [1024, D], F32, kind="ExternalInput").ap()
    wout = dt("w_out", [D, D], F32, kind="ExternalInput").ap()
    w2g = dt("ffn2_w_gate", [D, DFF], F32, kind="ExternalInput").ap()
    w2u = dt("ffn2_w_up", [D, DFF], F32, kind="ExternalInput").ap()
    w2d = dt("ffn2_w_down", [DFF, D], F32, kind="ExternalInput").ap()
    outT = dt("outT", [D, TOWN], F32, kind="ExternalOutput").ap()
    x1T = dt("x1T", [D, TPAD], F32, kind="Internal").ap()
    x1bT = dt("x1bT", [D, TPAD], BF16, kind="Internal").ap()
    kTd = dt("kTd", [512, TPAD], BF16, kind="Internal").ap()
    Vd = dt("Vd", [TPAD, 512], BF16, kind="Internal").ap()
    pTd = dt("pTd", [1024, TPAD], F32, kind="Internal").ap()
    mixTd = dt("mixTd", [1024, TOWN], BF16, kind="Internal").ap()
    attnTd = dt("attnTd", [D, TOWN], BF16, kind="Internal").ap()
    mrgTd = dt("mrgTd", [D, TOWN], BF16, kind="Internal").ap()
    x2T = dt("x2T", [D, TOWN], F32, kind="Internal").ap()
    x2bT = dt("x2bT", [D, TOWN], BF16, kind="Internal").ap()

    def fm(ap2d, c0, nchunk, t0, nt):
        return ap2d[c0 * 128:(c0 + nchunk) * 128, t0:t0 + nt].rearrange("(c p) t -> p c t", p=128)

    ARENA_W = 51200
    with ExitStack() as stack:
        ec = stack.enter_context
        arena = ec(nc.sbuf_tensor("arena", [128, ARENA_W], F32))
        cst = ec(nc.sbuf_tensor("cstsb", [128, NCST], F32))
        sexp = ec(nc.sbuf_tensor("sexp", [128, 16], F32))
        onesS = ec(nc.sbuf_tensor("onesS", [128, 128], BF16))
        ones1 = ec(nc.sbuf_tensor("ones1", [128, 128], BF16))
        pwsb = ec(nc.sbuf_tensor("pwsb", [128, 4, 2, 256], BF16))
        ps = ec(nc.psum_tensor("ps", [128, 8, 512], F32))
        P = Prog(nc, stack)

        def carve(off_words, nelem, dtype):
            nw = nelem if dtype == F32 else (nelem + 1) // 2
            assert off_words + nw <= ARENA_W, (off_words, nw)
            a = arena[:, off_words:off_words + nw]
            if dtype != F32:
                a = a.bitcast(dtype)
            return a, off_words + nw

        NSLOT = 6
        SLOT_W = 2048
        slot_base = ARENA_W - NSLOT * SLOT_W
        slots = []
        for i in range(NSLOT):
            a, _ = carve(slot_base + i * SLOT_W, 4096, BF16)
            slots.append((a, Buf("slot%d" % i)))
        slot_ctr = [0]
        TMP_W = 512 * 4 + 1024 * 2 + 1024 * 2
        tmp_base = slot_base - TMP_W
        o = tmp_base
        tmpf = []
        for i in range(4):
            a, o = carve(o, 512, F32)
            tmpf.append((a, Buf("tmpf%d" % i)))
        zbq = []
        for i in range(4):
            a, o = carve(o, 1024, BF16)
            zbq.append((a, Buf("zbq%d" % i)))
        mean_sb, o = carve(o, 1024, F32)
        rstd_sb, o = carve(o, 1024, F32)
        mean_b = Buf("mean")
        rstd_b = Buf("rstd")
        assert o == slot_base
        MAIN_W = tmp_base
        tmp_ctr = [0]

        banks = [Buf("bank%d" % i) for i in range(8)]
        bank_ctr = [0]

        reserved = set()

        def next_bank():
            while True:
                i = bank_ctr[0] % 8
                bank_ctr[0] += 1
                if i not in reserved:
                    break
            return ps[:, i, :], banks[i]

        def reserve_bank():
            ap_, b_ = next_bank()
            reserved.add(banks.index(b_))
            return ap_, b_

        def next_tmp():
            i = tmp_ctr[0] % 4
            tmp_ctr[0] += 1
            return tmpf[i]

        cst_b = Buf("cst")
        misc_b = Buf("misc")

        xslots = []
        for i in range(2):
            a, _ = carve(tmp_base + 2048 + i * SLOT_W, 4096, BF16)
            xslots.append((a, Buf("xslot%d" % i)))
        ring = [slots]

        prefetched = {}

        def prefetch_w(w2d, r0, kc, c0, ncol=256):
            key = (w2d.name, r0, kc, c0, ncol)
            assert key not in prefetched
            prefetched[key] = load_w(w2d, r0, kc, c0, ncol)

        def load_w(w2d, r0, kc, c0, ncol=256):
            key = (w2d.name, r0, kc, c0, ncol)
            if key in prefetched:
                return prefetched.pop(key)
            assert kc * ncol <= 4096
            a, b = ring[0][slot_ctr[0] % len(ring[0])]
            slot_ctr[0] += 1
            dst = a[:, 0:kc * ncol].rearrange("p (k n) -> p k n", k=kc)
            src = w2d[r0 * 128:(r0 + kc) * 128, c0:c0 + ncol].rearrange("(k p) n -> p k n", p=128)
            P.dma("pool", lambda g, dst=dst, src=src: g.dma_start(out=dst, in_=src), b, writes=[b])
            return dst, b

        def acc(parts, rhs_fn, n_out, extra_reads, U):
            bank_ap, bank_b = next_bank()
            total = sum(kc for _, _, kc in parts)
            idx = 0
            for (sa, sb, kc) in parts:
                for k in range(kc):
                    lhsT = sa[:, k, n_out * 128:(n_out + 1) * 128]
                    rhs = rhs_fn(idx)
                    first = idx == 0
                    last = idx == total - 1
                    P.op("pe", lambda t, o=bank_ap[:, 0:U], l=lhsT, r=rhs, f=first, la=last:
                         t.matmul(o, l, r, start=f, stop=la),
                         reads=[sb] + list(extra_reads), writes=[bank_b], signal=last)
                    idx += 1
            return bank_ap[:, 0:U], bank_b

        P.dma("sp", lambda s: s.dma_start(out=cst[:], in_=cstd[:, :]), cst_b, writes=[cst_b])
        P.op("dve", lambda v: v.memset(onesS[:], 1.0 / 2048.0), writes=[misc_b])
        P.op("dve", lambda v: v.memset(ones1[:], 1.0), writes=[misc_b])
        P.op("act", lambda a: a.activation(sexp[:], cst[:, C_SINK:C_SINK + 16], AF.Exp),
             reads=[cst_b], writes=[misc_b])
        pw_b = Buf("pw")
        P.dma("pool", lambda g: g.dma_start(out=pwsb[:], in_=pwg.rearrange("g (k p) n -> p g k n", p=128)),
              pw_b, writes=[pw_b])

        def ccol(c):
            return cst[:, c:c + 1]

        def ln_begin(units):
            nu = len(units)
            sb = [reserve_bank() for _ in range(nu)]
            qb = [reserve_bank() for _ in range(nu)]
            return (sb, qb)

        def ln_stats(st, ZF, zf_b, c, Tt, units):
            sb, qb = st
            za, zab = zbq[(2 * c) % 4]
            qa, qab = zbq[(2 * c + 1) % 4]
            P.op("act", lambda a, o=za[:, 0:Tt], i=ZF[:, c, :]: a.activation(o, i, AF.Copy),
                 reads=[zf_b[c]], writes=[zab])
            P.op("act", lambda a, o=qa[:, 0:Tt], i=ZF[:, c, :]: a.activation(o, i, AF.Square),
                 reads=[zf_b[c]], writes=[qab])
            for ui, (u0, U) in enumerate(units):
                last = (c == DC - 1)
                P.op("pe", lambda t, o=sb[ui][0][:, 0:U], r=za[:, u0:u0 + U], f=(c == 0), la=last:
                     t.matmul(o, onesS[:], r, start=f, stop=la),
                     reads=[zab, misc_b], writes=[sb[ui][1]], signal=True)
                P.op("pe", lambda t, o=qb[ui][0][:, 0:U], r=qa[:, u0:u0 + U], f=(c == 0), la=last:
                     t.matmul(o, onesS[:], r, start=f, stop=la),
                     reads=[qab, misc_b], writes=[qb[ui][1]], signal=True)

        def ln_finish_gen(st, ZF, zf_b, Tt, units, lncol, YB, yb_b, on_chunk=None):
            sb, qb = st
            for ui, (u0, U) in enumerate(units):
                m = mean_sb[:, u0:u0 + U]
                r = rstd_sb[:, u0:u0 + U]
                P.op("act", lambda a, o=m, i=sb[ui][0][:, 0:U]: a.activation(o, i, AF.Copy),
                     reads=[sb[ui][1]], writes=[mean_b])
                P.op("dve", lambda v, o=r, i=m: v.tensor_tensor(o, i, i, ALU.mult),
                     reads=[mean_b], writes=[rstd_b])
                P.op("dve", lambda v, o=r, i=qb[ui][0][:, 0:U]:
                     v.scalar_tensor_tensor(o, i, EPS_P, o, ALU.add, ALU.subtract),
                     reads=[qb[ui][1], rstd_b], writes=[rstd_b])
                P.op("act", lambda a, o=r: a.activation(o, o, AF.Sqrt), reads=[rstd_b], writes=[rstd_b])
                P.op("dve", lambda v, o=r: v.reciprocal(o, o), reads=[rstd_b], writes=[rstd_b])
            for ap_, b_ in sb + qb:
                reserved.discard(banks.index(b_))
            yield
            def e_sub(c):
                z = ZF[:, c, :]
                P.op("dve", lambda v, z=z: v.tensor_tensor(z, z, mean_sb[:, 0:Tt], ALU.subtract),
                     reads=[zf_b[c], mean_b], writes=[zf_b[c]])

            e_sub(0)
            for c in range(DC):
                z = ZF[:, c, :]
                if c + 1 < DC:
                    e_sub(c + 1)
                P.op("dve", lambda v, z=z: v.tensor_tensor(z, z, rstd_sb[:, 0:Tt], ALU.mult),
                     reads=[zf_b[c], rstd_b], writes=[zf_b[c]])
                g = ccol(lncol + c)
                b = ccol(lncol + 16 + c)
                P.op("act", lambda a, z=z, g=g, b=b: a.activation(z, z, AF.Identity, bias=b, scale=g),
                     reads=[zf_b[c], cst_b], writes=[zf_b[c]])
                if on_chunk is not None:
                    on_chunk(c)
                if YB is not None and c >= 1:
                    P.op("act", lambda a, o=YB[:, c - 1, :], z=ZF[:, c - 1, :]: a.activation(o, z, AF.Copy),
                         reads=[zf_b[c - 1]], writes=[yb_b])
                yield
            if YB is not None:
                P.op("act", lambda a, o=YB[:, DC - 1, :], z=ZF[:, DC - 1, :]: a.activation(o, z, AF.Copy),
                     reads=[zf_b[DC - 1]], writes=[yb_b])

        def ln_finish(*a, **k):
            for _ in ln_finish_gen(*a, **k):
                pass

        def ffn_phase(tiles, xf_d, xb_d, xb_cast, wg, wu, wd, lncol, outf_d, outb_d, hook, xb_preloaded=False):
            TtM = max(sum(U for _, U in units) for _, units in tiles)
            o = 0
            XBf_, o = carve(o, DC * TtM, BF16)
            ZFf_, o = carve(o, DC * TtM, F32)
            o_hook = o
            HTf_, o = carve(o, 22 * TtM, BF16)
            assert o <= MAIN_W, (o, MAIN_W)
            xb_b = Buf("xb")
            zf_ld = Buf("zfld")
            zf_b = [Buf("zf%d" % c) for c in range(DC)]
            ht_b = [Buf("ht%d" % c) for c in range(22)]
            st_b = Buf("zfst")
            hs_b = [Buf("hs%d" % i) for i in range(3)]
            def xb_view(Tt):
                return XBf_[:, 0:DC * Tt].rearrange("p (c t) -> p c t", c=DC)

            def xb_load(ti):
                t0_, units_ = tiles[ti]
                Tt_ = sum(U for _, U in units_)
                P.dma("sp", lambda s, o_=xb_view(Tt_), i_=fm(xb_d, 0, DC, t0_, Tt_): s.dma_start(out=o_, in_=i_),
                      xb_b, writes=[xb_b])

            def zf_load(ti):
                t0_, units_ = tiles[ti]
                Tt_ = sum(U for _, U in units_)
                ZF_ = ZFf_[:, 0:DC * Tt_].rearrange("p (c t) -> p c t", c=DC)
                P.dma("sp", lambda s, o_=ZF_, i_=fm(xf_d, 0, DC, t0_, Tt_): s.dma_start(out=o_, in_=i_),
                      zf_ld, writes=zf_b)

            deferred = [None]

            def pump(ti):
                if deferred[0] is not None:
                    try:
                        next(deferred[0])
                    except StopIteration:
                        deferred[0] = None
                        zf_load(ti)

            for ti, (t0, units) in enumerate(tiles):
                Tt = sum(U for _, U in units)
                XB = xb_view(Tt)
                ZF = ZFf_[:, 0:DC * Tt].rearrange("p (c t) -> p c t", c=DC)
                HT = HTf_[:, 0:22 * Tt].rearrange("p (c t) -> p c t", c=22)
                if ti == 0:
                    zf_load(0)
                if xb_cast:
                    for c in range(DC):
                        P.op("dve", lambda v, o_=XB[:, c, :], i_=ZF[:, c, :]: v.tensor_copy(o_, i_),
                             reads=[zf_b[c]], writes=[xb_b])
                elif ti == 0 and not xb_preloaded:
                    xb_load(0)
                for g in range(2):
                    for s in range(11):
                        col0 = g * 2816 + s * 256
                        sg = load_w(wg, 0, 16, col0)
                        su = load_w(wu, 0, 16, col0)
                        for nn in range(2):
                            f = s * 2 + nn
                            for (u0, U) in units:
                                rf = lambda k, u0=u0, U=U: XB[:, k, u0:u0 + U]
                                bg, bgb = acc([(sg[0], sg[1], 16)], rf, nn, [xb_b], U)
                                bu, bub = acc([(su[0], su[1], 16)], rf, nn, [xb_b], U)
                                ta, tb = next_tmp()
                                P.op("act", lambda a, o_=ta[:, 0:U], i_=bg: a.activation(o_, i_, AF.Silu),
                                     reads=[bgb], writes=[tb])
                                P.op("dve", lambda v, o_=HT[:, f, u0:u0 + U], i0=bu, i1=ta[:, 0:U]:
                                     v.tensor_tensor(o_, i0, i1, ALU.mult),
                                     reads=[bub, tb] + hs_b, writes=[ht_b[f]])
                            pump(ti)
                    if g == 1:
                        if (not xb_cast) and ti + 1 < len(tiles):
                            xb_load(ti + 1)
                        lnst = ln_begin(units)
                    for s in range(8):
                        sa = load_w(wd, g * 22, 11, s * 256)
                        sb_ = load_w(wd, g * 22 + 11, 11, s * 256)
                        for nn in range(2):
                            n = s * 2 + nn
                            for (u0, U) in units:
                                rf = lambda k, u0=u0, U=U: HT[:, k, u0:u0 + U]
                                bd, bdb = acc([(sa[0], sa[1], 11), (sb_[0], sb_[1], 11)], rf, nn, ht_b, U)
                                P.op("dve", lambda v, z=ZF[:, n, u0:u0 + U], i0=bd:
                                     v.scalar_tensor_tensor(z, i0, C_FFN, z, ALU.mult, ALU.add),
                                     reads=[bdb, zf_b[n]], writes=[zf_b[n]])
                            if g == 1 and n >= 2:
                                ln_stats(lnst, ZF, zf_b, n - 2, Tt, units)
                ln_stats(lnst, ZF, zf_b, DC - 2, Tt, units)
                ln_stats(lnst, ZF, zf_b, DC - 1, Tt, units)
                def store_chunk(c, ZF=ZF, t0=t0, Tt=Tt):
                    P.dma("sp", lambda s, i_=ZF[:, c, :],
                          o_=outf_d[0][c * 128:(c + 1) * 128, outf_d[1] + t0:outf_d[1] + t0 + Tt]:
                          s.dma_start(out=o_, in_=i_), st_b, reads=[zf_b[c]], writes=[outf_d[2]])

                assert deferred[0] is None
                if hook is None and outb_d is None and ti + 1 < len(tiles):
                    deferred[0] = ln_finish_gen(lnst, ZF, zf_b, Tt, units, lncol, None, xb_b, on_chunk=store_chunk)
                    continue
                ln_finish(lnst, ZF, zf_b, Tt, units, lncol, XB if outb_d is not None else None, xb_b,
                          on_chunk=store_chunk)
                if outb_d is not None:
                    P.dma("sp", lambda s, i_=XB, o_=fm(outb_d[0], 0, DC, t0, Tt): s.dma_start(out=o_, in_=i_),
                          xb_b, reads=[xb_b], writes=[outb_d[2]])
                if ti + 1 < len(tiles):
                    zf_load(ti + 1)
                if hook is not None:
                    hook(t0, Tt, units, XB, xb_b, o_hook, hs_b)
            P.barrier()

        kT_db = Buf("kTd")
        V_db = Buf("Vd")
        pT_db = Buf("pTd")
        x1T_db = Buf("x1T")
        x1bT_db = Buf("x1bT")

        kst_b, vst_b, pst_b = Buf("kst"), Buf("vst"), Buf("pst")

        def kvp_hook(t0, Tt, units, XB, xb_b, o2, hs_b):
            KSTf, o2 = carve(o2, 4 * Tt, BF16)
            KST = KSTf.rearrange("p (c t) -> p c t", c=4)
            nblk = Tt // 128
            VSTf, o2 = carve(o2, nblk * 512, BF16)
            VST = VSTf.rearrange("p (b d) -> p b d", b=nblk)
            PSTf, o2 = carve(o2, 8 * Tt, F32)
            PST = PSTf.rearrange("p (c t) -> p c t", c=8)
            assert o2 <= MAIN_W, (o2, MAIN_W)
            for s in range(4):
                sl = load_w(w_in, 0, 16, 3072 + s * 256)
                for nn in range(2):
                    n = s * 2 + nn
                    for (u0, U) in units:
                        rf = lambda k, u0=u0, U=U: XB[:, k, u0:u0 + U]
                        b, bb = acc([(sl[0], sl[1], 16)], rf, nn, [xb_b], U)
                        P.op("act", lambda a, o_=PST[:, n, u0:u0 + U], i_=b: a.activation(o_, i_, AF.Copy),
                             reads=[bb], writes=[pst_b])
            P.dma("sp", lambda s_, i_=PST, o_=fm(pTd, 0, 8, t0, Tt): s_.dma_start(out=o_, in_=i_),
                  pst_b, reads=[pst_b], writes=[pT_db, hs_b[2]])
            for s in range(2):
                sl = load_w(w_in, 0, 16, 2048 + s * 256)
                for nn in range(2):
                    n = s * 2 + nn
                    for (u0, U) in units:
                        rf = lambda k, u0=u0, U=U: XB[:, k, u0:u0 + U]
                        b, bb = acc([(sl[0], sl[1], 16)], rf, nn, [xb_b], U)
                        P.op("act", lambda a, o_=KST[:, n, u0:u0 + U], i_=b: a.activation(o_, i_, AF.Copy),
                             reads=[bb], writes=[kst_b])
            P.dma("sp", lambda s_, i_=KST, o_=fm(kTd, 0, 4, t0, Tt): s_.dma_start(out=o_, in_=i_),
                  kst_b, reads=[kst_b], writes=[kT_db, hs_b[0]])
            for half in range(2):
                sl = load_w(w_in, 0, 16, 2560 + half * 256)
                for bi in range(nblk):
                    bank_ap, bank_b = next_bank()
                    for k in range(16):
                        P.op("pe", lambda t, o_=bank_ap[:, 0:256], l=XB[:, k, bi * 128:(bi + 1) * 128],
                             r=sl[0][:, k, :], f=(k == 0), la=(k == 15): t.matmul(o_, l, r, start=f, stop=la),
                             reads=[sl[1], xb_b], writes=[bank_b], signal=(k == 15))
                    P.op("dve", lambda v, o_=VST[:, bi, half * 256:(half + 1) * 256], i_=bank_ap[:, 0:256]:
                         v.tensor_copy(o_, i_), reads=[bank_b], writes=[vst_b])
            P.dma("sp", lambda s_, i_=VST,
                  o_=Vd[t0:t0 + Tt, :].rearrange("(b p) d -> p b d", p=128): s_.dma_start(out=o_, in_=i_),
                  vst_b, reads=[vst_b], writes=[V_db, hs_b[1]])

        tilesA = [(0, [(0, 384), (384, 384)]), (768, [(0, 384), (384, 384)]), (1536, [(0, 384), (384, 384)])]
        ffn_phase(tilesA, xT, xT, True, w1g, w1u, w1d, C_LN + 0, (x1T, 0, x1T_db), (x1bT, 0, x1bT_db), kvp_hook)

        X1f_b2, _ = carve(MAIN_W - DC * 512, DC * 1024, BF16)
        X1_B2 = X1f_b2.rearrange("p (c t) -> p c t", c=DC)
        x1_b2 = Buf("x1b2")

        def x1_load_b2(h_):
            P.dma("sp", lambda s, o_=X1_B2, i_=fm(x1bT, 0, DC, HALO + h_ * 1024, 1024): s.dma_start(out=o_, in_=i_),
                  x1_b2, reads=[x1bT_db], writes=[x1_b2])

        mix_db = Buf("mixTd")
        for h in range(2):
            o = 0
            PFf, o = carve(o, 8 * 1040, F32)
            PF = PFf.rearrange("p (c t) -> p c t", c=8)
            S1, o = carve(o, 1040, F32)
            S2, o = carve(o, 1040, F32)
            S3, o = carve(o, 1040, F32)
            S4, o = carve(o, 1040, F32)
            DTf, o = carve(o, 8 * 1024, BF16)
            DTt = DTf.rearrange("p (c t) -> p c t", c=8)
            MTf, o = carve(o, 8 * 1024, BF16)
            MT = MTf.rearrange("p (c t) -> p c t", c=8)
            assert o <= MAIN_W
            pf_b = [Buf("pf%d" % c) for c in range(8)]
            pf_ld = Buf("pfld")
            s1_b, s2_b, s3_b, s4_b = Buf("s1"), Buf("s2"), Buf("s3"), Buf("s4")
            dt_b = [Buf("dt%d" % c) for c in range(8)]
            mt_b = Buf("mt")
            P.dma("sp", lambda s, o_=PF, i_=fm(pTd, 0, 8, HALO + h * 1024 - 8, 1040): s.dma_start(out=o_, in_=i_),
                  pf_ld, reads=[pT_db], writes=pf_b)
            for c in range(8):
                if h == 0:
                    P.op("dve", lambda v, a=PF[:, c, 0:8]: v.tensor_scalar(a, a, ccol(C_PMASK), None, ALU.mult),
                         reads=[pf_b[c], cst_b], writes=[pf_b[c]])
                else:
                    P.op("dve", lambda v, a=PF[:, c, 1032:1040]: v.tensor_scalar(a, a, ccol(C_PMASK + 1), None, ALU.mult),
                         reads=[pf_b[c], cst_b], writes=[pf_b[c]])
            for c in range(8):
                gi = c // 2
                w = (2, 4, 8, 16)[gi]
                p = PF[:, c, :]
                en = "dve"
                if c % 2 == 0:
                    SA, sab_, SB, sbb_ = S1, s1_b, S2, s2_b
                else:
                    SA, sab_, SB, sbb_ = S3, s3_b, S4, s4_b
                P.op(en, lambda v, p=p, SA=SA: v.tensor_tensor(SA[:, 0:1039], p[:, 0:1039], p[:, 1:1040], ALU.add),
                     reads=[pf_b[c]], writes=[sab_])
                cur, curb, oth, othb = SA, sab_, SB, sbb_
                n_valid = 1039
                step = 2
                while step < w:
                    nv = n_valid - step
                    P.op(en, lambda v, o_=oth[:, 0:nv], a=cur[:, 0:nv], b=cur[:, step:step + nv]:
                         v.tensor_tensor(o_, a, b, ALU.add), reads=[curb], writes=[othb])
                    cur, curb, oth, othb = oth, othb, cur, curb
                    n_valid = nv
                    step *= 2
                sh = 8 - w // 2
                en = "dve"
                P.op(en, lambda v, o_=DTt[:, c, :], a=cur[:, sh:sh + 1024], p=p[:, 8:1032], w=w:
                     v.scalar_tensor_tensor(o_, a, 1.0 / w, p, ALU.mult, ALU.subtract),
                     reads=[curb, pf_b[c]], writes=[dt_b[c]])
                if h == 0:
                    e0, tcol = 0, C_INVC + gi * 16
                else:
                    e0, tcol = 1016, C_INVC + gi * 16 + 8
                P.op(en, lambda v, o_=oth[:, 0:8], a=cur[:, sh + e0:sh + e0 + 8], t_=cst[:, tcol:tcol + 8]:
                     v.tensor_tensor(o_, a, t_, ALU.mult), reads=[curb, cst_b], writes=[othb])
                P.op(en, lambda v, o_=DTt[:, c, e0:e0 + 8], a=oth[:, 0:8], p=p[:, 8 + e0:16 + e0]:
                     v.tensor_tensor(o_, a, p, ALU.subtract), reads=[othb, pf_b[c], dt_b[c]], writes=[dt_b[c]])
            for gi in range(4):
                for no in range(2):
                    for u in range(2):
                        bank_ap, bank_b = next_bank()
                        for ki in range(2):
                            P.op("pe", lambda t, o_=bank_ap, l=pwsb[:, gi, ki, no * 128:(no + 1) * 128],
                                 r=DTt[:, 2 * gi + ki, u * 512:(u + 1) * 512], f=(ki == 0), la=(ki == 1):
                                 t.matmul(o_, l, r, start=f, stop=la),
                                 reads=[pw_b, dt_b[2 * gi + ki]], writes=[bank_b], signal=(ki == 1))
                        cc = 2 * gi + no
                        P.op("act", lambda a, o_=MT[:, cc, u * 512:(u + 1) * 512], i_=bank_ap, sc=ccol(C_PSC + cc):
                             a.activation(o_, i_, AF.Identity, scale=sc), reads=[bank_b, cst_b], writes=[mt_b])
            P.dma("sp", lambda s, i_=MT, o_=fm(mixTd, 0, 8, h * 1024, 1024): s.dma_start(out=o_, in_=i_),
                  mt_b, reads=[mt_b], writes=[mix_db])
            if h == 1:
                x1_load_b2(0)
                for g_ in range(3):
                    for s_ in range(2):
                        prefetch_w(w_in, 0, 16, g_ * 512 + s_ * 256)
            P.barrier()

        attn_db = Buf("attnTd")
        for h in range(2):
            o = 0
            X1 = X1_B2
            x1_b = x1_b2
            KTf, o = carve(o, 4 * 1280, BF16)
            KT = KTf.rearrange("p (c t) -> p c t", c=4)
            VVf, o = carve(o, 10 * 512, BF16)
            VV = VVf.rearrange("p (b d) -> p b d", b=10)
            ATf, o = carve(o, DC * 1024, BF16)
            AT = ATf.rearrange("p (c t) -> p c t", c=DC)
            QTs = []
            for i in range(2):
                qf, o = carve(o, 4 * 1024, BF16)
                QTs.append((qf.rearrange("p (c t) -> p c t", c=4), Buf("qt%d" % i)))
            PTs = []
            for i in range(8):
                pf_, o = carve(o, 512, BF16)
                PTs.append((pf_, Buf("pt%d" % i)))
            DNs = []
            for i in range(2):
                df_, o = carve(o, 512, F32)
                DNs.append((df_, Buf("dn%d" % i)))
            EGs = []
            for i in range(2):
                egf_, o = carve(o, 3 * 512, F32)
                sxg_, o = carve(o, 512, F32)
                EGs.append((egf_, sxg_, Buf("eg%d" % i), Buf("sxg%d" % i)))
            assert o <= MAIN_W - DC * 512, (o, MAIN_W)
            kt_b, vv_b = Buf("kt"), Buf("vv")
            at_g = [Buf("at%d" % g_) for g_ in range(4)]
            atst_b = Buf("atst")
            P.dma("sp", lambda s, o_=KT, i_=fm(kTd, 0, 4, h * 1024, 1280): s.dma_start(out=o_, in_=i_),
                  kt_b, reads=[kT_db], writes=[kt_b])
            P.dma("sp", lambda s, o_=VV,
                  i_=Vd[h * 1024:h * 1024 + 1280, :].rearrange("(b p) d -> p b d", p=128): s.dma_start(out=o_, in_=i_),
                  vv_b, reads=[V_db], writes=[vv_b])
            pt_ctr = [0]
            qslots = {}

            def qpiece(g, idx):
                QT, qt_b = QTs[g % 2]
                s_, nn, u = idx // 4, (idx // 2) % 2, idx % 2
                sl = qslots[(g, s_)]
                hd = s_ * 2 + nn
                rf = lambda k, u=u: X1[:, k, u * 512:(u + 1) * 512]
                b, bb = acc([(sl[0], sl[1], 16)], rf, nn, [x1_b], 512)
                P.op("dve", lambda v, o_=QT[:, hd, u * 512:(u + 1) * 512], i_=b: v.tensor_copy(o_, i_),
                     reads=[bb], writes=[qt_b])

            def qload(g):
                for s_ in range(2):
                    qslots[(g, s_)] = load_w(w_in, 0, 16, g * 512 + s_ * 256)

            qload(0)
            qload(1)
            qload(2)

            def s_stage(g, i):
                QT, qt_b = QTs[g % 2]
                pts = []
                for j in range(3):
                    blk = i + j
                    bank_ap, bank_b = next_bank()
                    bS = bank_ap.rearrange("p (h q) -> p h q", h=4)
                    P.op("pe", lambda t, o_=bS, l=KT[:, g, blk * 128:(blk + 1) * 128],
                         r=QT[:, :, i * 128:(i + 1) * 128]: t.matmul(o_, l, r, start=True, stop=True),
                         reads=[kt_b, qt_b], writes=[bank_b])
                    ta, tb = next_tmp()
                    P.op("act", lambda a, o_=ta, i_=bank_ap, km=ccol(C_KMASK + h * 8 + blk):
                         a.activation(o_, i_, AF.Exp, bias=km, scale=SCALE),
                         reads=[bank_b, cst_b], writes=[tb])
                    pa, pb = PTs[pt_ctr[0] % len(PTs)]
                    pt_ctr[0] += 1
                    P.op("pool" if j < 2 else "dve", lambda g_, o_=pa, i0=ta, i1=EGs[g % 2][0][:, j * 512:(j + 1) * 512]:
                         g_.tensor_tensor(o_, i0, i1, ALU.mult),
                         reads=[tb, EGs[g % 2][2]], writes=[pb])
                    pts.append((pa, pb, blk))
                return pts

            def o_stage(g, i, pts):
                bO_ap, bO_b = next_bank()
                for j, (pa, pb, blk) in enumerate(pts):
                    P.op("pe", lambda t, o_=bO_ap, l=VV[:, blk, g * 128:(g + 1) * 128], r=pa, f=(j == 0), la=(j == 2):
                         t.matmul(o_, l, r, start=f, stop=la),
                         reads=[vv_b, pb], writes=[bO_b], signal=(j == 2))
                bD_ap, bD_b = next_bank()
                for j, (pa, pb, blk) in enumerate(pts):
                    P.op("pe", lambda t, o_=bD_ap, r=pa, f=(j == 0), la=(j == 2):
                         t.matmul(o_, ones1[:], r, start=f, stop=la),
                         reads=[misc_b, pb], writes=[bD_b], signal=(j == 2))
                dn, dnb = DNs[i % 2]
                P.op("dve", lambda v, o_=dn, i0=bD_ap, i1=EGs[g % 2][1]: v.tensor_tensor(o_, i0, i1, ALU.add),
                     reads=[bD_b, EGs[g % 2][3]], writes=[dnb])
                P.op("act", lambda a, o_=dn: a.activation(o_, o_, AF.Ln), reads=[dnb], writes=[dnb])
                P.op("act", lambda a, o_=dn: a.activation(o_, o_, AF.Exp, scale=-1.0), reads=[dnb], writes=[dnb])
                P.op("dve", lambda v, o_=AT[:, g * 4:(g + 1) * 4, i * 128:(i + 1) * 128],
                     i0=bO_ap.rearrange("p (h q) -> p h q", h=4), i1=dn.rearrange("p (h q) -> p h q", h=4):
                     v.tensor_tensor(o_, i0, i1, ALU.mult),
                     reads=[bO_b, dnb], writes=[at_g[g]])
                if i == 7:
                    P.dma("sp", lambda s_, i_=AT[:, g * 4:(g + 1) * 4, :], o_=fm(attnTd, g * 4, 4, h * 1024, 1024):
                          s_.dma_start(out=o_, in_=i_), atst_b, reads=[at_g[g]], writes=[attn_db])

            def eg_build(g):
                egf_, sxg_, eg_b, sxg_b = EGs[g % 2]
                EG = egf_.rearrange("p (j h q) -> p j h q", j=3, h=4)
                SXG3 = sxg_.rearrange("p (h q) -> p h q", h=4)
                for j in range(3):
                    for hd in range(4):
                        P.op("act", lambda a, o_=EG[:, j, hd, :], nd=cst[:, C_NEGD + j * 128:C_NEGD + (j + 1) * 128],
                             sc=ccol(C_SLOPE2 + g * 4 + hd): a.activation(o_, nd, AF.Exp, scale=sc),
                             reads=[cst_b], writes=[eg_b])
                for hd in range(4):
                    P.op("act", lambda a, o_=SXG3[:, hd, :], nd=cst[:, C_NEGD + 128:C_NEGD + 256],
                         se=sexp[:, g * 4 + hd:g * 4 + hd + 1]: a.activation(o_, nd, AF.Identity, bias=se, scale=0.0),
                         reads=[cst_b, misc_b], writes=[sxg_b])

            for idx in range(8):
                qpiece(0, idx)
                if idx == 0:
                    eg_build(0)
            seq = [(g, i) for g in range(4) for i in range(8)]
            pts_next = s_stage(0, 0)
            for si, (g, i) in enumerate(seq):
                pts = pts_next
                if g == 0 and i == 4:
                    qload(3)
                if i == 1 and g < 3:
                    eg_build(g + 1)
                if si + 1 < len(seq):
                    pts_next = s_stage(*seq[si + 1])
                o_stage(g, i, pts)
                if g < 3 and i < 4:
                    qpiece(g + 1, 2 * i)
                    qpiece(g + 1, 2 * i + 1)
                    if g == 2 and i == 3 and h == 0:
                        x1_load_b2(1)
            if h == 0:
                for g_ in range(3):
                    for s_ in range(2):
                        prefetch_w(w_in, 0, 16, g_ * 512 + s_ * 256)
            else:
                ring[0] = slots + xslots
                for s_ in range(2):
                    prefetch_w(wpa, 0, 16, s_ * 256)
                    prefetch_w(wpp, 0, 8, s_ * 256)
                    prefetch_w(w_in, 0, 16, 4096 + s_ * 256)
                    prefetch_w(w_in, 0, 16, 6144 + s_ * 256)
            P.barrier()

        mrg_db = Buf("mrgTd")
        ring[0] = slots + xslots
        for h in range(2):
            o = 0
            ATf, o = carve(o, DC * 1024, BF16)
            AT = ATf.rearrange("p (c t) -> p c t", c=DC)
            MTf, o = carve(o, 8 * 1024, BF16)
            MT = MTf.rearrange("p (c t) -> p c t", c=8)
            X1f, o = carve(o, DC * 1024, BF16)
            X1 = X1f.rearrange("p (c t) -> p c t", c=DC)
            MGf, o = carve(o, DC * 1024, BF16)
            MG = MGf.rearrange("p (c t) -> p c t", c=DC)
            assert o <= MAIN_W
            at_b, mt_b, x1_b = Buf("at"), Buf("mt"), Buf("x1")
            mg_c = [Buf("mg%d" % c) for c in range(DC)]
            mgst_b = Buf("mgst")
            P.dma("sp", lambda s, o_=AT, i_=fm(attnTd, 0, DC, h * 1024, 1024): s.dma_start(out=o_, in_=i_),
                  at_b, reads=[attn_db], writes=[at_b])
            P.dma("sp", lambda s, o_=MT, i_=fm(mixTd, 0, 8, h * 1024, 1024): s.dma_start(out=o_, in_=i_),
                  mt_b, reads=[mix_db], writes=[mt_b])
            P.dma("sp", lambda s, o_=X1, i_=fm(x1bT, 0, DC, HALO + h * 1024, 1024): s.dma_start(out=o_, in_=i_),
                  x1_b, reads=[x1bT_db], writes=[x1_b])
            for s in range(8):
                s_pa = load_w(wpa, 0, 16, s * 256)
                s_pp = load_w(wpp, 0, 8, s * 256)
                s_ga = load_w(w_in, 0, 16, 4096 + s * 256)
                s_gb = load_w(w_in, 0, 16, 6144 + s * 256)
                for nn in range(2):
                    n = s * 2 + nn
                    for u in range(2):
                        us = slice(u * 512, (u + 1) * 512)
                        bya, byab = acc([(s_pa[0], s_pa[1], 16)], lambda k, us=us: AT[:, k, us], nn, [at_b], 512)
                        byb, bybb = acc([(s_pp[0], s_pp[1], 8)], lambda k, us=us: MT[:, k, us], nn, [mt_b], 512)
                        bga, bgab = acc([(s_ga[0], s_ga[1], 16)], lambda k, us=us: X1[:, k, us], nn, [x1_b], 512)
                        bgb, bgbb = acc([(s_gb[0], s_gb[1], 16)], lambda k, us=us: X1[:, k, us], nn, [x1_b], 512)
                        sa, sab = next_tmp()
                        sb2, sbb = next_tmp()
                        P.op("act", lambda a, o_=sa, i_=bga: a.activation(o_, i_, AF.Sigmoid), reads=[bgab], writes=[sab])
                        P.op("act", lambda a, o_=sb2, i_=bgb: a.activation(o_, i_, AF.Sigmoid), reads=[bgbb], writes=[sbb])
                        P.op("dve", lambda v, o_=sa, i0=bya: v.tensor_tensor(o_, i0, o_, ALU.mult),
                             reads=[byab, sab], writes=[sab])
                        P.op("dve", lambda v, o_=sb2, i0=byb: v.tensor_tensor(o_, i0, o_, ALU.mult),
                             reads=[bybb, sbb], writes=[sbb])
                        P.op("dve", lambda v, o_=MG[:, n, us], a_=sa, b_=sb2: v.tensor_tensor(o_, a_, b_, ALU.add),
                             reads=[sab, sbb], writes=[mg_c[n]])
                    P.dma("sp", lambda s_, i_=MG[:, n, :], o_=mrgTd[n * 128:(n + 1) * 128, h * 1024:(h + 1) * 1024]:
                          s_.dma_start(out=o_, in_=i_), mgst_b, reads=[mg_c[n]], writes=[mrg_db])
            if h == 0:
                for s_ in range(2):
                    prefetch_w(wpa, 0, 16, s_ * 256)
                    prefetch_w(wpp, 0, 8, s_ * 256)
                    prefetch_w(w_in, 0, 16, 4096 + s_ * 256)
                    prefetch_w(w_in, 0, 16, 6144 + s_ * 256)
            else:
                ring[0] = slots
                for s_ in range(4):
                    prefetch_w(wout, 0, 16, s_ * 256)
            P.barrier()

        x2T_db, x2bT_db = Buf("x2T"), Buf("x2bT")
        ring[0] = slots
        zf_ld4 = [Buf("zfld%d" % i) for i in range(4)]
        o = 0
        MGf, o = carve(o, DC * 1024, BF16)
        MG = MGf.rearrange("p (c t) -> p c t", c=DC)
        ZFf, o = carve(o, DC * 1024, F32)
        ZF = ZFf.rearrange("p (c t) -> p c t", c=DC)
        YBf, o = carve(o, DC * 1024, BF16)
        YB = YBf.rearrange("p (c t) -> p c t", c=DC)
        assert o <= MAIN_W
        mg_b, yb_b, st_b = Buf("mg"), Buf("yb"), Buf("zfst")
        zf_b = [Buf("zf%d" % c) for c in range(DC)]

        def mg_load(h_):
            P.dma("sp", lambda s, o_=MG, i_=fm(mrgTd, 0, DC, h_ * 1024, 1024): s.dma_start(out=o_, in_=i_),
                  mg_b, reads=[mrg_db], writes=[mg_b])

        mg_load(0)
        for h in range(2):
            for q4 in range(4):
                P.dma("sp" if h == 0 else "pool",
                      lambda s, o_=ZF[:, q4 * 4:(q4 + 1) * 4, :], i_=fm(x1T, q4 * 4, 4, HALO + h * 1024, 1024):
                      s.dma_start(out=o_, in_=i_), zf_ld4[q4], reads=[x1T_db], writes=zf_b[q4 * 4:(q4 + 1) * 4])
            units = [(0, 512), (512, 512)]
            lnst = ln_begin(units)
            for s in range(8):
                sl = load_w(wout, 0, 16, s * 256)
                for nn in range(2):
                    n = s * 2 + nn
                    for (u0, U) in units:
                        b, bb = acc([(sl[0], sl[1], 16)], lambda k, u0=u0, U=U: MG[:, k, u0:u0 + U], nn, [mg_b], U)
                        P.op("dve", lambda v, z=ZF[:, n, u0:u0 + U], i0=b:
                             v.scalar_tensor_tensor(z, i0, C_MIX, z, ALU.mult, ALU.add),
                             reads=[bb, zf_b[n]], writes=[zf_b[n]])
                    if n >= 2:
                        ln_stats(lnst, ZF, zf_b, n - 2, 1024, units)
            if h == 0:
                mg_load(1)
            else:
                xbc, _ = carve(0, DC * 768, BF16)
                P.dma("sp", lambda s, o_=xbc.rearrange("p (c t) -> p c t", c=DC), i_=fm(x2bT, 0, DC, 0, 768):
                      s.dma_start(out=o_, in_=i_), mg_b, reads=[x2bT_db], writes=[mg_b])
            ln_stats(lnst, ZF, zf_b, DC - 2, 1024, units)
            ln_stats(lnst, ZF, zf_b, DC - 1, 1024, units)

            def store_chunk4(c, h=h):
                P.dma("sp", lambda s, i_=ZF[:, c, :], o_=x2T[c * 128:(c + 1) * 128, h * 1024:(h + 1) * 1024]:
                      s.dma_start(out=o_, in_=i_), st_b, reads=[zf_b[c]], writes=[x2T_db])

            ln_finish(lnst, ZF, zf_b, 1024, units, C_LN + 32, YB, yb_b, on_chunk=store_chunk4)
            P.dma("sp", lambda s, i_=YB, o_=fm(x2bT, 0, DC, h * 1024, 1024): s.dma_start(out=o_, in_=i_),
                  yb_b, reads=[yb_b], writes=[x2bT_db])
            if h == 0:
                for s_ in range(4):
                    prefetch_w(wout, 0, 16, s_ * 256)
            else:
                for s_ in range(2):
                    prefetch_w(w2g, 0, 16, s_ * 256)
                    prefetch_w(w2u, 0, 16, s_ * 256)
        P.barrier()

        out_db = Buf("outT")
        tilesC = [(0, [(0, 384), (384, 384)]), (768, [(0, 384), (384, 384)]), (1536, [(0, 256), (256, 256)])]
        ffn_phase(tilesC, x2T, x2bT, False, w2g, w2u, w2d, C_LN + 64, (outT, 0, out_db), None, None,
                  xb_preloaded=True)
        P.barrier()
        assert not prefetched, list(prefetched)

        with nc.Block() as block:
            @block.tensor
            def _(t):
                P.replay("pe", t)

            @block.scalar
            def _(a):
                P.replay("act", a)

            @block.vector
            def _(v):
                P.replay("dve", v)

            @block.gpsimd
            def _(g):
                P.replay("pool", g)

            @block.sync
            def _(s):
                P.replay("sp", s)
    return nc


def _make_cst(core, inp):
    cst = np.zeros((128, NCST), np.float32)

    def fmcol(v):
        return np.ascontiguousarray(v.reshape(-1, 128).T)

    for i, nm in enumerate(("ln1_g", "ln1_b", "ln2_g", "ln2_b", "ln3_g", "ln3_b")):
        cst[:, C_LN + 16 * i:C_LN + 16 * (i + 1)] = fmcol(inp[nm][0])
    cst[:, C_PSC:C_PSC + 8] = fmcol(inp["pool_scale"][0])
    cst[:, C_SINK:C_SINK + 16] = np.broadcast_to(inp["attn_sink"][0][None, :], (128, 16))
    slopes = np.exp2(-8.0 * np.arange(1, 17, dtype=np.float32) / 16.0).astype(np.float32)
    cst[:, C_SLOPE:C_SLOPE + 16] = (slopes / np.float32(SCALE))[None, :]
    cst[:, C_SLOPE2:C_SLOPE2 + 16] = slopes[None, :]
    km = np.zeros(18, np.float32)
    if core == 0:
        km[0] = -30000.0
    if core == NCORES - 1:
        km[17] = -30000.0
    cst[:, C_KMASK:C_KMASK + 18] = km[None, :]
    cst[:, C_PMASK] = 0.0 if core == 0 else 1.0
    cst[:, C_PMASK + 1] = 0.0 if core == NCORES - 1 else 1.0
    for gi, w in enumerate((2, 4, 8, 16)):
        tl = np.arange(8)
        gt = core * TOWN + tl
        lo = np.clip(gt - w // 2, 0, SEQ)
        hi = np.clip(gt + w - w // 2, 0, SEQ)
        cst[:, C_INVC + gi * 16:C_INVC + gi * 16 + 8] = (1.0 / (hi - lo).astype(np.float32))[None, :]
        gt = core * TOWN + TOWN - 8 + tl
        lo = np.clip(gt - w // 2, 0, SEQ)
        hi = np.clip(gt + w - w // 2, 0, SEQ)
        cst[:, C_INVC + gi * 16 + 8:C_INVC + gi * 16 + 16] = (1.0 / (hi - lo).astype(np.float32))[None, :]
    s_ = np.arange(128)[:, None]
    t_ = np.arange(128)[None, :]
    for j in range(3):
        if j == 0:
            dist = t_ - s_ + 128
        elif j == 1:
            dist = np.abs(t_ - s_)
        else:
            dist = s_ + 128 - t_
        nd = np.where(dist <= 128, -dist.astype(np.float32), np.float32(NEG_BIG))
        cst[:, C_NEGD + j * 128:C_NEGD + (j + 1) * 128] = nd
    return cst


_NC_CACHE = {}


def kernel(**inputs):
    inp = {k: np.asarray(v) for k, v in inputs.items()}
    x = inp["x"][0]
    xpad = np.zeros((SEQ + 2 * HALO, D), np.float32)
    xpad[HALO:HALO + SEQ] = x
    if "nc" not in _NC_CACHE:
        _NC_CACHE["nc"] = build_program()
    nc = _NC_CACHE["nc"]
    wnames = ("ffn1_w_gate", "ffn1_w_up", "ffn1_w_down", "w_in", "pool_w_groups", "w_proj_attn",
              "w_proj_pool", "w_out", "ffn2_w_gate", "ffn2_w_up", "ffn2_w_down")
    wts = {n: np.ascontiguousarray(inp[n][0], dtype=np.float32) for n in wnames}
    in_maps = []
    for c in range(NCORES):
        m = dict(wts)
        m["xT"] = np.ascontiguousarray(xpad[c * TOWN:c * TOWN + TPAD].T)
        m["cst"] = _make_cst(c, inp)
        in_maps.append(m)
    res = run_bass_kernel_spmd(nc, in_maps, core_ids=list(range(NCORES)))
    out = np.concatenate([np.ascontiguousarray(r["outT"].T) for r in res.results], axis=0)
    return out.reshape(1, SEQ, D).astype(np.float32)
```

```python
import math
from contextlib import ExitStack

import numpy as np
import concourse.bass as bass
import concourse.mybir as mybir
from concourse.bass_utils import run_bass_kernel_spmd

F32 = mybir.dt.float32
BF16 = mybir.dt.bfloat16
AF = mybir.ActivationFunctionType
ALU = mybir.AluOpType

NCORES = 8
D = 2048
DC = 16
SEQ = 16384
TOWN = SEQ // NCORES
HALO = 128
TPAD = TOWN + 2 * HALO
DFF = 5632
FC = 44
INW = 8192
ALPHA = 2.0 ** 0.25
LN_EPS = 1e-5
EPS_P = LN_EPS / (ALPHA * ALPHA)
C_FFN = 0.5 / ALPHA
C_MIX = 1.0 / ALPHA
SCALE = 1.0 / math.sqrt(128.0)
NEG_BIG = -1.0e6

C_LN = 0
C_PSC = 96
C_SINK = 104
C_SLOPE = 120
C_KMASK = 136
C_PMASK = 154
C_INVC = 156
C_NEGD = 220
C_SLOPE2 = 604
NCST = 620

SAME_ENG_SYNC = True


class Buf:
    __slots__ = ("name", "lastw", "reads", "dsem", "dcount")

    def __init__(self, name):
        self.name = name
        self.lastw = None
        self.reads = {}
        self.dsem = None
        self.dcount = 0


class Eng:
    def __init__(self, name):
        self.name = name
        self.sem = None
        self.count = 0
        self.pending = False
        self.ops = []
        self.seen = {}


class Prog:
    def __init__(self, nc, stack):
        self.nc = nc
        self.stack = stack
        self.engs = {n: Eng(n) for n in ("pe", "act", "dve", "pool", "sp")}
        for n, e in self.engs.items():
            e.sem = stack.enter_context(nc.semaphore("s_" + n))
        self.all_sems = {}
        self.nsem = 0

    def _note(self, sem, val):
        k = id(sem)
        if k not in self.all_sems or self.all_sems[k][1] < val:
            self.all_sems[k] = (sem, val)

    def _waits(self, eng, reads, writes):
        need = {}

        def add(tok):
            if tok is None:
                return
            sem, val = tok
            k = id(sem)
            if k not in need or need[k][1] < val:
                need[k] = (sem, val)

        for b in reads:
            add(b.lastw)
        for b in writes:
            add(b.lastw)
            for k, tok in b.reads.items():
                add(tok)
        out = []
        for k, (sem, val) in need.items():
            if sem is eng.sem and not (SAME_ENG_SYNC and eng.name != "pe"):
                continue
            if eng.seen.get(k, 0) >= val:
                continue
            eng.seen[k] = val
            out.append((sem, val))
        return out

    def _commit(self, tok, reads, writes):
        k = id(tok[0])
        for b in reads:
            if b.reads.get(k, (None, 0))[1] < tok[1]:
                b.reads[k] = tok
        for b in writes:
            b.lastw = tok
            b.reads = {}
        self._note(*tok)

    def op(self, engname, fn, reads=(), writes=(), signal=True):
        eng = self.engs[engname]
        waits = self._waits(eng, reads, writes)
        if signal:
            eng.count += 1
            eng.pending = False
            tok = (eng.sem, eng.count)
            inc = (eng.sem, 1)
        else:
            eng.pending = True
            tok = (eng.sem, eng.count + 1)
            inc = None
        eng.ops.append((waits, fn, inc))
        self._commit(tok, reads, writes)
        return tok

    def dma(self, engname, fn, owner, reads=(), writes=()):
        eng = self.engs[engname]
        if owner.dsem is None:
            owner.dsem = self.stack.enter_context(self.nc.semaphore("d_%d" % self.nsem))
            self.nsem += 1
        waits = self._waits(eng, reads, writes)
        owner.dcount += 16
        tok = (owner.dsem, owner.dcount)
        eng.ops.append((waits, fn, (owner.dsem, 16)))
        self._commit(tok, reads, writes)
        return tok

    def barrier(self):
        for e in self.engs.values():
            assert not e.pending, e.name
        toks = list(self.all_sems.values())
        for e in self.engs.values():
            waits = []
            for sem, val in toks:
                if sem is e.sem:
                    continue
                if e.seen.get(id(sem), 0) >= val:
                    continue
                e.seen[id(sem)] = val
                waits.append((sem, val))
            if waits:
                e.ops.append((waits, None, None))

    def replay(self, engname, handle):
        for waits, fn, inc in self.engs[engname].ops:
            for sem, val in waits:
                handle.wait_ge(sem, val)
            if fn is not None:
                ins = fn(handle)
                if inc is not None:
                    ins.then_inc(inc[0], inc[1])


def build_program():
    nc = bass.Bass("TRN2", target_bir_lowering=False)
    dt = nc.dram_tensor
    xT = dt("xT", [D, TPAD], F32, kind="ExternalInput").ap()
    cstd = dt("cst", [128, NCST], F32, kind="ExternalInput").ap()
    w1g = dt("ffn1_w_gate", [D, DFF], F32, kind="ExternalInput").ap()
    w1u = dt("ffn1_w_up", [D, DFF], F32, kind="ExternalInput").ap()
    w1d = dt("ffn1_w_down", [DFF, D], F32, kind="ExternalInput").ap()
    w_in = dt("w_in", [D, INW], F32, kind="ExternalInput").ap()
    pwg = dt("pool_w_groups", [4, 256, 256], F32, kind="ExternalInput").ap()
    wpa = dt("w_proj_attn", [D, D], F32, kind="ExternalInput").ap()
    wpp = dt("w_proj_pool", [1024, D], F32, kind="ExternalInput").ap()
    wout = dt("w_out", [D, D], F32, kind="ExternalInput").ap()
    w2g = dt("ffn2_w_gate", [D, DFF], F32, kind="ExternalInput").ap()
    w2u = dt("ffn2_w_up", [D, DFF], F32, kind="ExternalInput").ap()
    w2d = dt("ffn2_w_down", [DFF, D], F32, kind="ExternalInput").ap()
    outT = dt("outT", [D, TOWN], F32, kind="ExternalOutput").ap()
    x1T = dt("x1T", [D, TPAD], F32, kind="Internal").ap()
    x1bT = dt("x1bT", [D, TPAD], BF16, kind="Internal").ap()
    kTd = dt("kTd", [512, TPAD], BF16, kind="Internal").ap()
    Vd = dt("Vd", [TPAD, 512], BF16, kind="Internal").ap()
    pTd = dt("pTd", [1024, TPAD], F32, kind="Internal").ap()
    mixTd = dt("mixTd", [1024, TOWN], BF16, kind="Internal").ap()
    attnTd = dt("attnTd", [D, TOWN], BF16, kind="Internal").ap()
    mrgTd = dt("mrgTd", [D, TOWN], BF16, kind="Internal").ap()
    x2T = dt("x2T", [D, TOWN], F32, kind="Internal").ap()
    x2bT = dt("x2bT", [D, TOWN], BF16, kind="Internal").ap()

    def fm(ap2d, c0, nchunk, t0, nt):
        return ap2d[c0 * 128:(c0 + nchunk) * 128, t0:t0 + nt].rearrange("(c p) t -> p c t", p=128)

    ARENA_W = 51200
    with ExitStack() as stack:
        ec = stack.enter_context
        arena = ec(nc.sbuf_tensor("arena", [128, ARENA_W], F32))
        cst = ec(nc.sbuf_tensor("cstsb", [128, NCST], F32))
        sexp = ec(nc.sbuf_tensor("sexp", [128, 16], F32))
        onesS = ec(nc.sbuf_tensor("onesS", [128, 128], BF16))
        ones1 = ec(nc.sbuf_tensor("ones1", [128, 128], BF16))
        pwsb = ec(nc.sbuf_tensor("pwsb", [128, 4, 2, 256], BF16))
        ps = ec(nc.psum_tensor("ps", [128, 8, 512], F32))
        P = Prog(nc, stack)

        def carve(off_words, nelem, dtype):
            nw = nelem if dtype == F32 else (nelem + 1) // 2
            assert off_words + nw <= ARENA_W, (off_words, nw)
            a = arena[:, off_words:off_words + nw]
            if dtype != F32:
                a = a.bitcast(dtype)
            return a, off_words + nw

        NSLOT = 6
        SLOT_W = 2048
        slot_base = ARENA_W - NSLOT * SLOT_W
        slots = []
        for i in range(NSLOT):
            a, _ = carve(slot_base + i * SLOT_W, 4096, BF16)
            slots.append((a, Buf("slot%d" % i)))
        slot_ctr = [0]
        TMP_W = 512 * 4 + 1024 * 2 + 1024 * 2
        tmp_base = slot_base - TMP_W
        o = tmp_base
        tmpf = []
        for i in range(4):
            a, o = carve(o, 512, F32)
            tmpf.append((a, Buf("tmpf%d" % i)))
        zbq = []
        for i in range(4):
            a, o = carve(o, 1024, BF16)
            zbq.append((a, Buf("zbq%d" % i)))
        mean_sb, o = carve(o, 1024, F32)
        rstd_sb, o = carve(o, 1024, F32)
        mean_b = Buf("mean")
        rstd_b = Buf("rstd")
        assert o == slot_base
        MAIN_W = tmp_base
        tmp_ctr = [0]

        banks = [Buf("bank%d" % i) for i in range(8)]
        bank_ctr = [0]

        reserved = set()

        def next_bank():
            while True:
                i = bank_ctr[0] % 8
                bank_ctr[0] += 1
                if i not in reserved:
                    break
            return ps[:, i, :], banks[i]

        def reserve_bank():
            ap_, b_ = next_bank()
            reserved.add(banks.index(b_))
            return ap_, b_

        def next_tmp():
            i = tmp_ctr[0] % 4
            tmp_ctr[0] += 1
            return tmpf[i]

        cst_b = Buf("cst")
        misc_b = Buf("misc")

        xslots = []
        for i in range(2):
            a, _ = carve(tmp_base + 2048 + i * SLOT_W, 4096, BF16)
            xslots.append((a, Buf("xslot%d" % i)))
        ring = [slots]

        prefetched = {}

        def prefetch_w(w2d, r0, kc, c0, ncol=256):
            key = (w2d.name, r0, kc, c0, ncol)
            assert key not in prefetched
            prefetched[key] = load_w(w2d, r0, kc, c0, ncol)

        def load_w(w2d, r0, kc, c0, ncol=256):
            key = (w2d.name, r0, kc, c0, ncol)
            if key in prefetched:
                return prefetched.pop(key)
            assert kc * ncol <= 4096
            a, b = ring[0][slot_ctr[0] % len(ring[0])]
            slot_ctr[0] += 1
            dst = a[:, 0:kc * ncol].rearrange("p (k n) -> p k n", k=kc)
            src = w2d[r0 * 128:(r0 + kc) * 128, c0:c0 + ncol].rearrange("(k p) n -> p k n", p=128)
            P.dma("pool", lambda g, dst=dst, src=src: g.dma_start(out=dst, in_=src), b, writes=[b])
            return dst, b

        def acc(parts, rhs_fn, n_out, extra_reads, U):
            bank_ap, bank_b = next_bank()
            total = sum(kc for _, _, kc in parts)
            idx = 0
            for (sa, sb, kc) in parts:
                for k in range(kc):
                    lhsT = sa[:, k, n_out * 128:(n_out + 1) * 128]
                    rhs = rhs_fn(idx)
                    first = idx == 0
                    last = idx == total - 1
                    P.op("pe", lambda t, o=bank_ap[:, 0:U], l=lhsT, r=rhs, f=first, la=last:
                         t.matmul(o, l, r, start=f, stop=la),
                         reads=[sb] + list(extra_reads), writes=[bank_b], signal=last)
                    idx += 1
            return bank_ap[:, 0:U], bank_b

        P.dma("sp", lambda s: s.dma_start(out=cst[:], in_=cstd[:, :]), cst_b, writes=[cst_b])
        P.op("dve", lambda v: v.memset(onesS[:], 1.0 / 2048.0), writes=[misc_b])
        P.op("dve", lambda v: v.memset(ones1[:], 1.0), writes=[misc_b])
        P.op("act", lambda a: a.activation(sexp[:], cst[:, C_SINK:C_SINK + 16], AF.Exp),
             reads=[cst_b], writes=[misc_b])
        pw_b = Buf("pw")
        P.dma("pool", lambda g: g.dma_start(out=pwsb[:], in_=pwg.rearrange("g (k p) n -> p g k n", p=128)),
              pw_b, writes=[pw_b])

        def ccol(c):
            return cst[:, c:c + 1]

        def ln_begin(units):
            nu = len(units)
            sb = [reserve_bank() for _ in range(nu)]
            qb = [reserve_bank() for _ in range(nu)]
            return (sb, qb)

        def ln_stats(st, ZF, zf_b, c, Tt, units):
            sb, qb = st
            za, zab = zbq[(2 * c) % 4]
            qa, qab = zbq[(2 * c + 1) % 4]
            P.op("act", lambda a, o=za[:, 0:Tt], i=ZF[:, c, :]: a.activation(o, i, AF.Copy),
                 reads=[zf_b[c]], writes=[zab])
            P.op("act", lambda a, o=qa[:, 0:Tt], i=ZF[:, c, :]: a.activation(o, i, AF.Square),
                 reads=[zf_b[c]], writes=[qab])
            for ui, (u0, U) in enumerate(units):
                last = (c == DC - 1)
                P.op("pe", lambda t, o=sb[ui][0][:, 0:U], r=za[:, u0:u0 + U], f=(c == 0), la=last:
                     t.matmul(o, onesS[:], r, start=f, stop=la),
                     reads=[zab, misc_b], writes=[sb[ui][1]], signal=True)
                P.op("pe", lambda t, o=qb[ui][0][:, 0:U], r=qa[:, u0:u0 + U], f=(c == 0), la=last:
                     t.matmul(o, onesS[:], r, start=f, stop=la),
                     reads=[qab, misc_b], writes=[qb[ui][1]], signal=True)

        def ln_finish_gen(st, ZF, zf_b, Tt, units, lncol, YB, yb_b, on_chunk=None):
            sb, qb = st
            for ui, (u0, U) in enumerate(units):
                m = mean_sb[:, u0:u0 + U]
                r = rstd_sb[:, u0:u0 + U]
                P.op("act", lambda a, o=m, i=sb[ui][0][:, 0:U]: a.activation(o, i, AF.Copy),
                     reads=[sb[ui][1]], writes=[mean_b])
                P.op("dve", lambda v, o=r, i=m: v.tensor_tensor(o, i, i, ALU.mult),
                     reads=[mean_b], writes=[rstd_b])
                P.op("dve", lambda v, o=r, i=qb[ui][0][:, 0:U]:
                     v.scalar_tensor_tensor(o, i, EPS_P, o, ALU.add, ALU.subtract),
                     reads=[qb[ui][1], rstd_b], writes=[rstd_b])
                P.op("act", lambda a, o=r: a.activation(o, o, AF.Sqrt), reads=[rstd_b], writes=[rstd_b])
                P.op("dve", lambda v, o=r: v.reciprocal(o, o), reads=[rstd_b], writes=[rstd_b])
            for ap_, b_ in sb + qb:
                reserved.discard(banks.index(b_))
            yield
            def e_sub(c):
                z = ZF[:, c, :]
                P.op("dve", lambda v, z=z: v.tensor_tensor(z, z, mean_sb[:, 0:Tt], ALU.subtract),
                     reads=[zf_b[c], mean_b], writes=[zf_b[c]])

            e_sub(0)
            for c in range(DC):
                z = ZF[:, c, :]
                if c + 1 < DC:
                    e_sub(c + 1)
                P.op("dve", lambda v, z=z: v.tensor_tensor(z, z, rstd_sb[:, 0:Tt], ALU.mult),
                     reads=[zf_b[c], rstd_b], writes=[zf_b[c]])
                g = ccol(lncol + c)
                b = ccol(lncol + 16 + c)
                P.op("act", lambda a, z=z, g=g, b=b: a.activation(z, z, AF.Identity, bias=b, scale=g),
                     reads=[zf_b[c], cst_b], writes=[zf_b[c]])
                if on_chunk is not None:
                    on_chunk(c)
                if YB is not None and c >= 1:
                    P.op("act", lambda a, o=YB[:, c - 1, :], z=ZF[:, c - 1, :]: a.activation(o, z, AF.Copy),
                         reads=[zf_b[c - 1]], writes=[yb_b])
                yield
            if YB is not None:
                P.op("act", lambda a, o=YB[:, DC - 1, :], z=ZF[:, DC - 1, :]: a.activation(o, z, AF.Copy),
                     reads=[zf_b[DC - 1]], writes=[yb_b])

        def ln_finish(*a, **k):
            for _ in ln_finish_gen(*a, **k):
                pass

        def ffn_phase(tiles, xf_d, xb_d, xb_cast, wg, wu, wd, lncol, outf_d, outb_d, hook, xb_preloaded=False):
            TtM = max(sum(U for _, U in units) for _, units in tiles)
            o = 0
            XBf_, o = carve(o, DC * TtM, BF16)
            ZFf_, o = carve(o, DC * TtM, F32)
            o_hook = o
            HTf_, o = carve(o, 22 * TtM, BF16)
            assert o <= MAIN_W, (o, MAIN_W)
            xb_b = Buf("xb")
            zf_ld = Buf("zfld")
            zf_b = [Buf("zf%d" % c) for c in range(DC)]
            ht_b = [Buf("ht%d" % c) for c in range(22)]
            st_b = Buf("zfst")
            hs_b = [Buf("hs%d" % i) for i in range(3)]
            def xb_view(Tt):
                return XBf_[:, 0:DC * Tt].rearrange("p (c t) -> p c t", c=DC)

            def xb_load(ti):
                t0_, units_ = tiles[ti]
                Tt_ = sum(U for _, U in units_)
                P.dma("sp", lambda s, o_=xb_view(Tt_), i_=fm(xb_d, 0, DC, t0_, Tt_): s.dma_start(out=o_, in_=i_),
                      xb_b, writes=[xb_b])

            def zf_load(ti):
                t0_, units_ = tiles[ti]
                Tt_ = sum(U for _, U in units_)
                ZF_ = ZFf_[:, 0:DC * Tt_].rearrange("p (c t) -> p c t", c=DC)
                P.dma("sp", lambda s, o_=ZF_, i_=fm(xf_d, 0, DC, t0_, Tt_): s.dma_start(out=o_, in_=i_),
                      zf_ld, writes=zf_b)

            deferred = [None]

            def pump(ti):
                if deferred[0] is not None:
                    try:
                        next(deferred[0])
                    except StopIteration:
                        deferred[0] = None
                        zf_load(ti)

            for ti, (t0, units) in enumerate(tiles):
                Tt = sum(U for _, U in units)
                XB = xb_view(Tt)
                ZF = ZFf_[:, 0:DC * Tt].rearrange("p (c t) -> p c t", c=DC)
                HT = HTf_[:, 0:22 * Tt].rearrange("p (c t) -> p c t", c=22)
                if ti == 0:
                    zf_load(0)
                if xb_cast:
                    for c in range(DC):
                        P.op("dve", lambda v, o_=XB[:, c, :], i_=ZF[:, c, :]: v.tensor_copy(o_, i_),
                             reads=[zf_b[c]], writes=[xb_b])
                elif ti == 0 and not xb_preloaded:
                    xb_load(0)
                for g in range(2):
                    for s in range(11):
                        col0 = g * 2816 + s * 256
                        sg = load_w(wg, 0, 16, col0)
                        su = load_w(wu, 0, 16, col0)
                        for nn in range(2):
                            f = s * 2 + nn
                            for (u0, U) in units:
                                rf = lambda k, u0=u0, U=U: XB[:, k, u0:u0 + U]
                                bg, bgb = acc([(sg[0], sg[1], 16)], rf, nn, [xb_b], U)
                                bu, bub = acc([(su[0], su[1], 16)], rf, nn, [xb_b], U)
                                ta, tb = next_tmp()
                                P.op("act", lambda a, o_=ta[:, 0:U], i_=bg: a.activation(o_, i_, AF.Silu),
                                     reads=[bgb], writes=[tb])
                                P.op("dve", lambda v, o_=HT[:, f, u0:u0 + U], i0=bu, i1=ta[:, 0:U]:
                                     v.tensor_tensor(o_, i0, i1, ALU.mult),
                                     reads=[bub, tb] + hs_b, writes=[ht_b[f]])
                            pump(ti)
                    if g == 1:
                        if (not xb_cast) and ti + 1 < len(tiles):
                            xb_load(ti + 1)
                        lnst = ln_begin(units)
                    for s in range(8):
                        sa = load_w(wd, g * 22, 11, s * 256)
                        sb_ = load_w(wd, g * 22 + 11, 11, s * 256)
                        for nn in range(2):
                            n = s * 2 + nn
                            for (u0, U) in units:
                                rf = lambda k, u0=u0, U=U: HT[:, k, u0:u0 + U]
                                bd, bdb = acc([(sa[0], sa[1], 11), (sb_[0], sb_[1], 11)], rf, nn, ht_b, U)
                                P.op("dve", lambda v, z=ZF[:, n, u0:u0 + U], i0=bd:
                                     v.scalar_tensor_tensor(z, i0, C_FFN, z, ALU.mult, ALU.add),
                                     reads=[bdb, zf_b[n]], writes=[zf_b[n]])
                            if g == 1 and n >= 2:
                                ln_stats(lnst, ZF, zf_b, n - 2, Tt, units)
                ln_stats(lnst, ZF, zf_b, DC - 2, Tt, units)
                ln_stats(lnst, ZF, zf_b, DC - 1, Tt, units)
                def store_chunk(c, ZF=ZF, t0=t0, Tt=Tt):
                    P.dma("sp", lambda s, i_=ZF[:, c, :],
                          o_=outf_d[0][c * 128:(c + 1) * 128, outf_d[1] + t0:outf_d[1] + t0 + Tt]:
                          s.dma_start(out=o_, in_=i_), st_b, reads=[zf_b[c]], writes=[outf_d[2]])

                assert deferred[0] is None
                if hook is None and outb_d is None and ti + 1 < len(tiles):
                    deferred[0] = ln_finish_gen(lnst, ZF, zf_b, Tt, units, lncol, None, xb_b, on_chunk=store_chunk)
                    continue
                ln_finish(lnst, ZF, zf_b, Tt, units, lncol, XB if outb_d is not None else None, xb_b,
                          on_chunk=store_chunk)
                if outb_d is not None:
                    P.dma("sp", lambda s, i_=XB, o_=fm(outb_d[0], 0, DC, t0, Tt): s.dma_start(out=o_, in_=i_),
                          xb_b, reads=[xb_b], writes=[outb_d[2]])
                if ti + 1 < len(tiles):
                    zf_load(ti + 1)
                if hook is not None:
                    hook(t0, Tt, units, XB, xb_b, o_hook, hs_b)
            P.barrier()

        kT_db = Buf("kTd")
        V_db = Buf("Vd")
        pT_db = Buf("pTd")
        x1T_db = Buf("x1T")
        x1bT_db = Buf("x1bT")

        kst_b, vst_b, pst_b = Buf("kst"), Buf("vst"), Buf("pst")

        def kvp_hook(t0, Tt, units, XB, xb_b, o2, hs_b):
            KSTf, o2 = carve(o2, 4 * Tt, BF16)
            KST = KSTf.rearrange("p (c t) -> p c t", c=4)
            nblk = Tt // 128
            VSTf, o2 = carve(o2, nblk * 512, BF16)
            VST = VSTf.rearrange("p (b d) -> p b d", b=nblk)
            PSTf, o2 = carve(o2, 8 * Tt, F32)
            PST = PSTf.rearrange("p (c t) -> p c t", c=8)
            assert o2 <= MAIN_W, (o2, MAIN_W)
            for s in range(4):
                sl = load_w(w_in, 0, 16, 3072 + s * 256)
                for nn in range(2):
                    n = s * 2 + nn
                    for (u0, U) in units:
                        rf = lambda k, u0=u0, U=U: XB[:, k, u0:u0 + U]
                        b, bb = acc([(sl[0], sl[1], 16)], rf, nn, [xb_b], U)
                        P.op("act", lambda a, o_=PST[:, n, u0:u0 + U], i_=b: a.activation(o_, i_, AF.Copy),
                             reads=[bb], writes=[pst_b])
            P.dma("sp", lambda s_, i_=PST, o_=fm(pTd, 0, 8, t0, Tt): s_.dma_start(out=o_, in_=i_),
                  pst_b, reads=[pst_b], writes=[pT_db, hs_b[2]])
            for s in range(2):
                sl = load_w(w_in, 0, 16, 2048 + s * 256)
                for nn in range(2):
                    n = s * 2 + nn
                    for (u0, U) in units:
                        rf = lambda k, u0=u0, U=U: XB[:, k, u0:u0 + U]
                        b, bb = acc([(sl[0], sl[1], 16)], rf, nn, [xb_b], U)
                        P.op("act", lambda a, o_=KST[:, n, u0:u0 + U], i_=b: a.activation(o_, i_, AF.Copy),
                             reads=[bb], writes=[kst_b])
            P.dma("sp", lambda s_, i_=KST, o_=fm(kTd, 0, 4, t0, Tt): s_.dma_start(out=o_, in_=i_),
                  kst_b, reads=[kst_b], writes=[kT_db, hs_b[0]])
            for half in range(2):
                sl = load_w(w_in, 0, 16, 2560 + half * 256)
                for bi in range(nblk):
                    bank_ap, bank_b = next_bank()
                    for k in range(16):
                        P.op("pe", lambda t, o_=bank_ap[:, 0:256], l=XB[:, k, bi * 128:(bi + 1) * 128],
                             r=sl[0][:, k, :], f=(k == 0), la=(k == 15): t.matmul(o_, l, r, start=f, stop=la),
                             reads=[sl[1], xb_b], writes=[bank_b], signal=(k == 15))
                    P.op("dve", lambda v, o_=VST[:, bi, half * 256:(half + 1) * 256], i_=bank_ap[:, 0:256]:
                         v.tensor_copy(o_, i_), reads=[bank_b], writes=[vst_b])
            P.dma("sp", lambda s_, i_=VST,
                  o_=Vd[t0:t0 + Tt, :].rearrange("(b p) d -> p b d", p=128): s_.dma_start(out=o_, in_=i_),
                  vst_b, reads=[vst_b], writes=[V_db, hs_b[1]])

        tilesA = [(0, [(0, 384), (384, 384)]), (768, [(0, 384), (384, 384)]), (1536, [(0, 384), (384, 384)])]
        ffn_phase(tilesA, xT, xT, True, w1g, w1u, w1d, C_LN + 0, (x1T, 0, x1T_db), (x1bT, 0, x1bT_db), kvp_hook)

        X1f_b2, _ = carve(MAIN_W - DC * 512, DC * 1024, BF16)
        X1_B2 = X1f_b2.rearrange("p (c t) -> p c t", c=DC)
        x1_b2 = Buf("x1b2")

        def x1_load_b2(h_):
            P.dma("sp", lambda s, o_=X1_B2, i_=fm(x1bT, 0, DC, HALO + h_ * 1024, 1024): s.dma_start(out=o_, in_=i_),
                  x1_b2, reads=[x1bT_db], writes=[x1_b2])

        mix_db = Buf("mixTd")
        for h in range(2):
            o = 0
            PFf, o = carve(o, 8 * 1040, F32)
            PF = PFf.rearrange("p (c t) -> p c t", c=8)
            S1, o = carve(o, 1040, F32)
            S2, o = carve(o, 1040, F32)
            S3, o = carve(o, 1040, F32)
            S4, o = carve(o, 1040, F32)
            DTf, o = carve(o, 8 * 1024, BF16)
            DTt = DTf.rearrange("p (c t) -> p c t", c=8)
            MTf, o = carve(o, 8 * 1024, BF16)
            MT = MTf.rearrange("p (c t) -> p c t", c=8)
            assert o <= MAIN_W
            pf_b = [Buf("pf%d" % c) for c in range(8)]
            pf_ld = Buf("pfld")
            s1_b, s2_b, s3_b, s4_b = Buf("s1"), Buf("s2"), Buf("s3"), Buf("s4")
            dt_b = [Buf("dt%d" % c) for c in range(8)]
            mt_b = Buf("mt")
            P.dma("sp", lambda s, o_=PF, i_=fm(pTd, 0, 8, HALO + h * 1024 - 8, 1040): s.dma_start(out=o_, in_=i_),
                  pf_ld, reads=[pT_db], writes=pf_b)
            for c in range(8):
                if h == 0:
                    P.op("dve", lambda v, a=PF[:, c, 0:8]: v.tensor_scalar(a, a, ccol(C_PMASK), None, ALU.mult),
                         reads=[pf_b[c], cst_b], writes=[pf_b[c]])
                else:
                    P.op("dve", lambda v, a=PF[:, c, 1032:1040]: v.tensor_scalar(a, a, ccol(C_PMASK + 1), None, ALU.mult),
                         reads=[pf_b[c], cst_b], writes=[pf_b[c]])
            for c in range(8):
                gi = c // 2
                w = (2, 4, 8, 16)[gi]
                p = PF[:, c, :]
                en = "dve"
                if c % 2 == 0:
                    SA, sab_, SB, sbb_ = S1, s1_b, S2, s2_b
                else:
                    SA, sab_, SB, sbb_ = S3, s3_b, S4, s4_b
                P.op(en, lambda v, p=p, SA=SA: v.tensor_tensor(SA[:, 0:1039], p[:, 0:1039], p[:, 1:1040], ALU.add),
                     reads=[pf_b[c]], writes=[sab_])
                cur, curb, oth, othb = SA, sab_, SB, sbb_
                n_valid = 1039
                step = 2
                while step < w:
                    nv = n_valid - step
                    P.op(en, lambda v, o_=oth[:, 0:nv], a=cur[:, 0:nv], b=cur[:, step:step + nv]:
                         v.tensor_tensor(o_, a, b, ALU.add), reads=[curb], writes=[othb])
                    cur, curb, oth, othb = oth, othb, cur, curb
                    n_valid = nv
                    step *= 2
                sh = 8 - w // 2
                en = "dve"
                P.op("act", lambda a, o_=oth[:, 0:1024], i_=cur[:, sh:sh + 1024], w=w:
                     a.activation(o_, i_, AF.Identity, scale=1.0 / w), reads=[curb], writes=[othb])
                P.op(en, lambda v, o_=DTt[:, c, :], a=oth[:, 0:1024], p=p[:, 8:1032]:
                     v.tensor_tensor(o_, a, p, ALU.subtract),
                     reads=[othb, pf_b[c]], writes=[dt_b[c]])
                if h == 0:
                    e0, tcol = 0, C_INVC + gi * 16
                else:
                    e0, tcol = 1016, C_INVC + gi * 16 + 8
                P.op(en, lambda v, o_=oth[:, 0:8], a=cur[:, sh + e0:sh + e0 + 8], t_=cst[:, tcol:tcol + 8]:
                     v.tensor_tensor(o_, a, t_, ALU.mult), reads=[curb, cst_b], writes=[othb])
                P.op(en, lambda v, o_=DTt[:, c, e0:e0 + 8], a=oth[:, 0:8], p=p[:, 8 + e0:16 + e0]:
                     v.tensor_tensor(o_, a, p, ALU.subtract), reads=[othb, pf_b[c], dt_b[c]], writes=[dt_b[c]])
            for gi in range(4):
                for no in range(2):
                    for u in range(2):
                        bank_ap, bank_b = next_bank()
                        for ki in range(2):
                            P.op("pe", lambda t, o_=bank_ap, l=pwsb[:, gi, ki, no * 128:(no + 1) * 128],
                                 r=DTt[:, 2 * gi + ki, u * 512:(u + 1) * 512], f=(ki == 0), la=(ki == 1):
                                 t.matmul(o_, l, r, start=f, stop=la),
                                 reads=[pw_b, dt_b[2 * gi + ki]], writes=[bank_b], signal=(ki == 1))
                        cc = 2 * gi + no
                        P.op("act", lambda a, o_=MT[:, cc, u * 512:(u + 1) * 512], i_=bank_ap, sc=ccol(C_PSC + cc):
                             a.activation(o_, i_, AF.Identity, scale=sc), reads=[bank_b, cst_b], writes=[mt_b])
            P.dma("sp", lambda s, i_=MT, o_=fm(mixTd, 0, 8, h * 1024, 1024): s.dma_start(out=o_, in_=i_),
                  mt_b, reads=[mt_b], writes=[mix_db])
            if h == 1:
                x1_load_b2(0)
                for g_ in range(3):
                    for s_ in range(2):
                        prefetch_w(w_in, 0, 16, g_ * 512 + s_ * 256)
            P.barrier()

        attn_db = Buf("attnTd")
        for h in range(2):
            o = 0
            X1 = X1_B2
            x1_b = x1_b2
            KTf, o = carve(o, 4 * 1280, BF16)
            KT = KTf.rearrange("p (c t) -> p c t", c=4)
            VVf, o = carve(o, 10 * 512, BF16)
            VV = VVf.rearrange("p (b d) -> p b d", b=10)
            ATf, o = carve(o, DC * 1024, BF16)
            AT = ATf.rearrange("p (c t) -> p c t", c=DC)
            QTs = []
            for i in range(2):
                qf, o = carve(o, 4 * 1024, BF16)
                QTs.append((qf.rearrange("p (c t) -> p c t", c=4), Buf("qt%d" % i)))
            PTs = []
            for i in range(8):
                pf_, o = carve(o, 512, BF16)
                PTs.append((pf_, Buf("pt%d" % i)))
            DNs = []
            for i in range(2):
                df_, o = carve(o, 512, F32)
                DNs.append((df_, Buf("dn%d" % i)))
            EGs = []
            for i in range(2):
                egf_, o = carve(o, 3 * 512, F32)
                sxg_, o = carve(o, 512, F32)
                EGs.append((egf_, sxg_, Buf("eg%d" % i), Buf("sxg%d" % i)))
            assert o <= MAIN_W - DC * 512, (o, MAIN_W)
            kt_b, vv_b = Buf("kt"), Buf("vv")
            at_g = [Buf("at%d" % g_) for g_ in range(4)]
            atst_b = Buf("atst")
            P.dma("sp", lambda s, o_=KT, i_=fm(kTd, 0, 4, h * 1024, 1280): s.dma_start(out=o_, in_=i_),
                  kt_b, reads=[kT_db], writes=[kt_b])
            P.dma("sp", lambda s, o_=VV,
                  i_=Vd[h * 1024:h * 1024 + 1280, :].rearrange("(b p) d -> p b d", p=128): s.dma_start(out=o_, in_=i_),
                  vv_b, reads=[V_db], writes=[vv_b])
            pt_ctr = [0]
            qslots = {}

            def qpiece(g, idx):
                QT, qt_b = QTs[g % 2]
                s_, nn, u = idx // 4, (idx // 2) % 2, idx % 2
                sl = qslots[(g, s_)]
                hd = s_ * 2 + nn
                rf = lambda k, u=u: X1[:, k, u * 512:(u + 1) * 512]
                b, bb = acc([(sl[0], sl[1], 16)], rf, nn, [x1_b], 512)
                P.op("dve", lambda v, o_=QT[:, hd, u * 512:(u + 1) * 512], i_=b: v.tensor_copy(o_, i_),
                     reads=[bb], writes=[qt_b])

            def qload(g):
                for s_ in range(2):
                    qslots[(g, s_)] = load_w(w_in, 0, 16, g * 512 + s_ * 256)

            qload(0)
            qload(1)
            qload(2)

            def s_stage(g, i):
                QT, qt_b = QTs[g % 2]
                pts = []
                for j in range(3):
                    blk = i + j
                    bank_ap, bank_b = next_bank()
                    bS = bank_ap.rearrange("p (h q) -> p h q", h=4)
                    P.op("pe", lambda t, o_=bS, l=KT[:, g, blk * 128:(blk + 1) * 128],
                         r=QT[:, :, i * 128:(i + 1) * 128]: t.matmul(o_, l, r, start=True, stop=True),
                         reads=[kt_b, qt_b], writes=[bank_b])
                    ta, tb = next_tmp()
                    P.op("act", lambda a, o_=ta, i_=bank_ap, km=ccol(C_KMASK + h * 8 + blk):
                         a.activation(o_, i_, AF.Exp, bias=km, scale=SCALE),
                         reads=[bank_b, cst_b], writes=[tb])
                    pa, pb = PTs[pt_ctr[0] % len(PTs)]
                    pt_ctr[0] += 1
                    P.op("pool" if j < 2 else "dve", lambda g_, o_=pa, i0=ta, i1=EGs[g % 2][0][:, j * 512:(j + 1) * 512]:
                         g_.tensor_tensor(o_, i0, i1, ALU.mult),
                         reads=[tb, EGs[g % 2][2]], writes=[pb])
                    pts.append((pa, pb, blk))
                return pts

            def o_stage(g, i, pts):
                bO_ap, bO_b = next_bank()
                for j, (pa, pb, blk) in enumerate(pts):
                    P.op("pe", lambda t, o_=bO_ap, l=VV[:, blk, g * 128:(g + 1) * 128], r=pa, f=(j == 0), la=(j == 2):
                         t.matmul(o_, l, r, start=f, stop=la),
                         reads=[vv_b, pb], writes=[bO_b], signal=(j == 2))
                bD_ap, bD_b = next_bank()
                for j, (pa, pb, blk) in enumerate(pts):
                    P.op("pe", lambda t, o_=bD_ap, r=pa, f=(j == 0), la=(j == 2):
                         t.matmul(o_, ones1[:], r, start=f, stop=la),
                         reads=[misc_b, pb], writes=[bD_b], signal=(j == 2))
                dn, dnb = DNs[i % 2]
                P.op("dve", lambda v, o_=dn, i0=bD_ap, i1=EGs[g % 2][1]: v.tensor_tensor(o_, i0, i1, ALU.add),
                     reads=[bD_b, EGs[g % 2][3]], writes=[dnb])
                P.op("act", lambda a, o_=dn: a.activation(o_, o_, AF.Ln), reads=[dnb], writes=[dnb])
                P.op("act", lambda a, o_=dn: a.activation(o_, o_, AF.Exp, scale=-1.0), reads=[dnb], writes=[dnb])
                P.op("dve", lambda v, o_=AT[:, g * 4:(g + 1) * 4, i * 128:(i + 1) * 128],
                     i0=bO_ap.rearrange("p (h q) -> p h q", h=4), i1=dn.rearrange("p (h q) -> p h q", h=4):
                     v.tensor_tensor(o_, i0, i1, ALU.mult),
                     reads=[bO_b, dnb], writes=[at_g[g]])
                if i == 7:
                    P.dma("sp", lambda s_, i_=AT[:, g * 4:(g + 1) * 4, :], o_=fm(attnTd, g * 4, 4, h * 1024, 1024):
                          s_.dma_start(out=o_, in_=i_), atst_b, reads=[at_g[g]], writes=[attn_db])

            def eg_build(g):
                egf_, sxg_, eg_b, sxg_b = EGs[g % 2]
                EG = egf_.rearrange("p (j h q) -> p j h q", j=3, h=4)
                SXG3 = sxg_.rearrange("p (h q) -> p h q", h=4)
                for j in range(3):
                    for hd in range(4):
                        P.op("act", lambda a, o_=EG[:, j, hd, :], nd=cst[:, C_NEGD + j * 128:C_NEGD + (j + 1) * 128],
                             sc=ccol(C_SLOPE2 + g * 4 + hd): a.activation(o_, nd, AF.Exp, scale=sc),
                             reads=[cst_b], writes=[eg_b])
                for hd in range(4):
                    P.op("act", lambda a, o_=SXG3[:, hd, :], nd=cst[:, C_NEGD + 128:C_NEGD + 256],
                         se=sexp[:, g * 4 + hd:g * 4 + hd + 1]: a.activation(o_, nd, AF.Identity, bias=se, scale=0.0),
                         reads=[cst_b, misc_b], writes=[sxg_b])

            for idx in range(8):
                qpiece(0, idx)
                if idx == 0:
                    eg_build(0)
            seq = [(g, i) for g in range(4) for i in range(8)]
            pts_next = s_stage(0, 0)
            for si, (g, i) in enumerate(seq):
                pts = pts_next
                if g == 0 and i == 4:
                    qload(3)
                if i == 1 and g < 3:
                    eg_build(g + 1)
                if si + 1 < len(seq):
                    pts_next = s_stage(*seq[si + 1])
                o_stage(g, i, pts)
                if g < 3 and i < 4:
                    qpiece(g + 1, 2 * i)
                    qpiece(g + 1, 2 * i + 1)
                    if g == 2 and i == 3 and h == 0:
                        x1_load_b2(1)
            if h == 0:
                for g_ in range(3):
                    for s_ in range(2):
                        prefetch_w(w_in, 0, 16, g_ * 512 + s_ * 256)
            else:
                ring[0] = slots + xslots
                for s_ in range(2):
                    prefetch_w(wpa, 0, 16, s_ * 256)
                    prefetch_w(wpp, 0, 8, s_ * 256)
                    prefetch_w(w_in, 0, 16, 4096 + s_ * 256)
                    prefetch_w(w_in, 0, 16, 6144 + s_ * 256)
            P.barrier()

        mrg_db = Buf("mrgTd")
        ring[0] = slots + xslots
        for h in range(2):
            o = 0
            ATf, o = carve(o, DC * 1024, BF16)
            AT = ATf.rearrange("p (c t) -> p c t", c=DC)
            MTf, o = carve(o, 8 * 1024, BF16)
            MT = MTf.rearrange("p (c t) -> p c t", c=8)
            X1f, o = carve(o, DC * 1024, BF16)
            X1 = X1f.rearrange("p (c t) -> p c t", c=DC)
            MGf, o = carve(o, DC * 1024, BF16)
            MG = MGf.rearrange("p (c t) -> p c t", c=DC)
            assert o <= MAIN_W
            at_b, mt_b, x1_b = Buf("at"), Buf("mt"), Buf("x1")
            mg_c = [Buf("mg%d" % c) for c in range(DC)]
            mgst_b = Buf("mgst")
            P.dma("sp", lambda s, o_=AT, i_=fm(attnTd, 0, DC, h * 1024, 1024): s.dma_start(out=o_, in_=i_),
                  at_b, reads=[attn_db], writes=[at_b])
            P.dma("sp", lambda s, o_=MT, i_=fm(mixTd, 0, 8, h * 1024, 1024): s.dma_start(out=o_, in_=i_),
                  mt_b, reads=[mix_db], writes=[mt_b])
            P.dma("sp", lambda s, o_=X1, i_=fm(x1bT, 0, DC, HALO + h * 1024, 1024): s.dma_start(out=o_, in_=i_),
                  x1_b, reads=[x1bT_db], writes=[x1_b])
            for s in range(8):
                s_pa = load_w(wpa, 0, 16, s * 256)
                s_pp = load_w(wpp, 0, 8, s * 256)
                s_ga = load_w(w_in, 0, 16, 4096 + s * 256)
                s_gb = load_w(w_in, 0, 16, 6144 + s * 256)
                for nn in range(2):
                    n = s * 2 + nn
                    for u in range(2):
                        us = slice(u * 512, (u + 1) * 512)
                        bya, byab = acc([(s_pa[0], s_pa[1], 16)], lambda k, us=us: AT[:, k, us], nn, [at_b], 512)
                        byb, bybb = acc([(s_pp[0], s_pp[1], 8)], lambda k, us=us: MT[:, k, us], nn, [mt_b], 512)
                        bga, bgab = acc([(s_ga[0], s_ga[1], 16)], lambda k, us=us: X1[:, k, us], nn, [x1_b], 512)
                        bgb, bgbb = acc([(s_gb[0], s_gb[1], 16)], lambda k, us=us: X1[:, k, us], nn, [x1_b], 512)
                        sa, sab = next_tmp()
                        sb2, sbb = next_tmp()
                        P.op("act", lambda a, o_=sa, i_=bga: a.activation(o_, i_, AF.Sigmoid), reads=[bgab], writes=[sab])
                        P.op("act", lambda a, o_=sb2, i_=bgb: a.activation(o_, i_, AF.Sigmoid), reads=[bgbb], writes=[sbb])
                        P.op("dve", lambda v, o_=sa, i0=bya: v.tensor_tensor(o_, i0, o_, ALU.mult),
                             reads=[byab, sab], writes=[sab])
                        P.op("dve", lambda v, o_=sb2, i0=byb: v.tensor_tensor(o_, i0, o_, ALU.mult),
                             reads=[bybb, sbb], writes=[sbb])
                        P.op("dve", lambda v, o_=MG[:, n, us], a_=sa, b_=sb2: v.tensor_tensor(o_, a_, b_, ALU.add),
                             reads=[sab, sbb], writes=[mg_c[n]])
                    P.dma("sp", lambda s_, i_=MG[:, n, :], o_=mrgTd[n * 128:(n + 1) * 128, h * 1024:(h + 1) * 1024]:
                          s_.dma_start(out=o_, in_=i_), mgst_b, reads=[mg_c[n]], writes=[mrg_db])
            if h == 0:
                for s_ in range(2):
                    prefetch_w(wpa, 0, 16, s_ * 256)
                    prefetch_w(wpp, 0, 8, s_ * 256)
                    prefetch_w(w_in, 0, 16, 4096 + s_ * 256)
                    prefetch_w(w_in, 0, 16, 6144 + s_ * 256)
            else:
                ring[0] = slots
                for s_ in range(4):
                    prefetch_w(wout, 0, 16, s_ * 256)
            P.barrier()

        x2T_db, x2bT_db = Buf("x2T"), Buf("x2bT")
        ring[0] = slots
        zf_ld4 = [Buf("zfld%d" % i) for i in range(4)]
        o = 0
        MGf, o = carve(o, DC * 1024, BF16)
        MG = MGf.rearrange("p (c t) -> p c t", c=DC)
        ZFf, o = carve(o, DC * 1024, F32)
        ZF = ZFf.rearrange("p (c t) -> p c t", c=DC)
        YBf, o = carve(o, DC * 1024, BF16)
        YB = YBf.rearrange("p (c t) -> p c t", c=DC)
        assert o <= MAIN_W
        mg_b, yb_b, st_b = Buf("mg"), Buf("yb"), Buf("zfst")
        zf_b = [Buf("zf%d" % c) for c in range(DC)]

        def mg_load(h_):
            P.dma("sp", lambda s, o_=MG, i_=fm(mrgTd, 0, DC, h_ * 1024, 1024): s.dma_start(out=o_, in_=i_),
                  mg_b, reads=[mrg_db], writes=[mg_b])

        mg_load(0)
        for h in range(2):
            for q4 in range(4):
                P.dma("sp" if h == 0 else "pool",
                      lambda s, o_=ZF[:, q4 * 4:(q4 + 1) * 4, :], i_=fm(x1T, q4 * 4, 4, HALO + h * 1024, 1024):
                      s.dma_start(out=o_, in_=i_), zf_ld4[q4], reads=[x1T_db], writes=zf_b[q4 * 4:(q4 + 1) * 4])
            units = [(0, 512), (512, 512)]
            lnst = ln_begin(units)
            for s in range(8):
                sl = load_w(wout, 0, 16, s * 256)
                for nn in range(2):
                    n = s * 2 + nn
                    for (u0, U) in units:
                        b, bb = acc([(sl[0], sl[1], 16)], lambda k, u0=u0, U=U: MG[:, k, u0:u0 + U], nn, [mg_b], U)
                        P.op("dve", lambda v, z=ZF[:, n, u0:u0 + U], i0=b:
                             v.scalar_tensor_tensor(z, i0, C_MIX, z, ALU.mult, ALU.add),
                             reads=[bb, zf_b[n]], writes=[zf_b[n]])
                    if n >= 2:
                        ln_stats(lnst, ZF, zf_b, n - 2, 1024, units)
            if h == 0:
                mg_load(1)
            else:
                xbc, _ = carve(0, DC * 768, BF16)
                P.dma("sp", lambda s, o_=xbc.rearrange("p (c t) -> p c t", c=DC), i_=fm(x2bT, 0, DC, 0, 768):
                      s.dma_start(out=o_, in_=i_), mg_b, reads=[x2bT_db], writes=[mg_b])
            ln_stats(lnst, ZF, zf_b, DC - 2, 1024, units)
            ln_stats(lnst, ZF, zf_b, DC - 1, 1024, units)

            def store_chunk4(c, h=h):
                P.dma("sp", lambda s, i_=ZF[:, c, :], o_=x2T[c * 128:(c + 1) * 128, h * 1024:(h + 1) * 1024]:
                      s.dma_start(out=o_, in_=i_), st_b, reads=[zf_b[c]], writes=[x2T_db])

            ln_finish(lnst, ZF, zf_b, 1024, units, C_LN + 32, YB, yb_b, on_chunk=store_chunk4)
            P.dma("sp", lambda s, i_=YB, o_=fm(x2bT, 0, DC, h * 1024, 1024): s.dma_start(out=o_, in_=i_),
                  yb_b, reads=[yb_b], writes=[x2bT_db])
            if h == 0:
                for s_ in range(4):
                    prefetch_w(wout, 0, 16, s_ * 256)
            else:
                for s_ in range(2):
                    prefetch_w(w2g, 0, 16, s_ * 256)
                    prefetch_w(w2u, 0, 16, s_ * 256)
        P.barrier()

        out_db = Buf("outT")
        tilesC = [(0, [(0, 384), (384, 384)]), (768, [(0, 384), (384, 384)]), (1536, [(0, 256), (256, 256)])]
        ffn_phase(tilesC, x2T, x2bT, False, w2g, w2u, w2d, C_LN + 64, (outT, 0, out_db), None, None,
                  xb_preloaded=True)
        P.barrier()
        assert not prefetched, list(prefetched)

        with nc.Block() as block:
            @block.tensor
            def _(t):
                P.replay("pe", t)

            @block.scalar
            def _(a):
                P.replay("act", a)

            @block.vector
            def _(v):
                P.replay("dve", v)

            @block.gpsimd
            def _(g):
                P.replay("pool", g)

            @block.sync
            def _(s):
                P.replay("sp", s)
    return nc


def _make_cst(core, inp):
    cst = np.zeros((128, NCST), np.float32)

    def fmcol(v):
        return np.ascontiguousarray(v.reshape(-1, 128).T)

    for i, nm in enumerate(("ln1_g", "ln1_b", "ln2_g", "ln2_b", "ln3_g", "ln3_b")):
        cst[:, C_LN + 16 * i:C_LN + 16 * (i + 1)] = fmcol(inp[nm][0])
    cst[:, C_PSC:C_PSC + 8] = fmcol(inp["pool_scale"][0])
    cst[:, C_SINK:C_SINK + 16] = np.broadcast_to(inp["attn_sink"][0][None, :], (128, 16))
    slopes = np.exp2(-8.0 * np.arange(1, 17, dtype=np.float32) / 16.0).astype(np.float32)
    cst[:, C_SLOPE:C_SLOPE + 16] = (slopes / np.float32(SCALE))[None, :]
    cst[:, C_SLOPE2:C_SLOPE2 + 16] = slopes[None, :]
    km = np.zeros(18, np.float32)
    if core == 0:
        km[0] = -30000.0
    if core == NCORES - 1:
        km[17] = -30000.0
    cst[:, C_KMASK:C_KMASK + 18] = km[None, :]
    cst[:, C_PMASK] = 0.0 if core == 0 else 1.0
    cst[:, C_PMASK + 1] = 0.0 if core == NCORES - 1 else 1.0
    for gi, w in enumerate((2, 4, 8, 16)):
        tl = np.arange(8)
        gt = core * TOWN + tl
        lo = np.clip(gt - w // 2, 0, SEQ)
        hi = np.clip(gt + w - w // 2, 0, SEQ)
        cst[:, C_INVC + gi * 16:C_INVC + gi * 16 + 8] = (1.0 / (hi - lo).astype(np.float32))[None, :]
        gt = core * TOWN + TOWN - 8 + tl
        lo = np.clip(gt - w // 2, 0, SEQ)
        hi = np.clip(gt + w - w // 2, 0, SEQ)
        cst[:, C_INVC + gi * 16 + 8:C_INVC + gi * 16 + 16] = (1.0 / (hi - lo).astype(np.float32))[None, :]
    s_ = np.arange(128)[:, None]
    t_ = np.arange(128)[None, :]
    for j in range(3):
        if j == 0:
            dist = t_ - s_ + 128
        elif j == 1:
            dist = np.abs(t_ - s_)
        else:
            dist = s_ + 128 - t_
        nd = np.where(dist <= 128, -dist.astype(np.float32), np.float32(NEG_BIG))
        cst[:, C_NEGD + j * 128:C_NEGD + (j + 1) * 128] = nd
    return cst


_NC_CACHE = {}


def kernel(**inputs):
    inp = {k: np.asarray(v) for k, v in inputs.items()}
    x = inp["x"][0]
    xpad = np.zeros((SEQ + 2 * HALO, D), np.float32)
    xpad[HALO:HALO + SEQ] = x
    if "nc" not in _NC_CACHE:
        _NC_CACHE["nc"] = build_program()
    nc = _NC_CACHE["nc"]
    wnames = ("ffn1_w_gate", "ffn1_w_up", "ffn1_w_down", "w_in", "pool_w_groups", "w_proj_attn",
              "w_proj_pool", "w_out", "ffn2_w_gate", "ffn2_w_up", "ffn2_w_down")
    wts = {n: np.ascontiguousarray(inp[n][0], dtype=np.float32) for n in wnames}
    in_maps = []
    for c in range(NCORES):
        m = dict(wts)
        m["xT"] = np.ascontiguousarray(xpad[c * TOWN:c * TOWN + TPAD].T)
        m["cst"] = _make_cst(c, inp)
        in_maps.append(m)
    res = run_bass_kernel_spmd(nc, in_maps, core_ids=list(range(NCORES)))
    out = np.concatenate([np.ascontiguousarray(r["outT"].T) for r in res.results], axis=0)
    return out.reshape(1, SEQ, D).astype(np.float32)
```
